# Optimizing a Trainium2 kernel written in Bass

```python
import jax, jax.numpy as jnp
from jax import lax
import numpy as np

D_MODEL = 1024
BATCH = 32
SEQ = 2048
DEPTH = 2

HEAD_DIM = 64
W_FOX = (D_MODEL * 3 // 8) // HEAD_DIM * HEAD_DIM
N_HEADS_FOX = W_FOX // HEAD_DIM
CONV_CH = D_MODEL // 4
W_DIL = D_MODEL - W_FOX - CONV_CH
N_HEADS_DIL = W_DIL // HEAD_DIM
DILATION_PAIRS = ((128, 1), (512, 4), (2048, 16))
CONV_K = 31
FFN_CONV_K = 3
D_FF = ((8 * D_MODEL // 3 + 127) // 128) * 128
Q_BLOCK = 128
EPS = 1e-6
OFF_QA = 0
OFF_KA = OFF_QA + W_FOX
OFF_VA = OFF_KA + W_FOX
OFF_FA = OFF_VA + W_FOX
OFF_QB = OFF_FA + N_HEADS_FOX
OFF_KB = OFF_QB + W_DIL
OFF_VB = OFF_KB + W_DIL
OFF_GV = OFF_VB + W_DIL
OFF_GG = OFF_GV + CONV_CH
N_IN = OFF_GG + CONV_CH

kernel_name = 'hybrid_fox_dilated_conformer_convffn'


def rmsnorm(x, g):
    xf = x.astype(jnp.float32)
    y = xf * lax.rsqrt(jnp.mean(xf * xf, axis=-1, keepdims=True) + EPS)
    return (y * g.astype(jnp.float32)).astype(x.dtype)


def layernorm(x, g, b):
    xf = x.astype(jnp.float32)
    mu = jnp.mean(xf, axis=-1, keepdims=True)
    xc = xf - mu
    y = xc * lax.rsqrt(jnp.mean(xc * xc, axis=-1, keepdims=True) + EPS)
    return (y * g.astype(jnp.float32) + b.astype(jnp.float32)).astype(x.dtype)


def causal_dwconv(x, w, b):
    k, c = w.shape
    y = lax.conv_general_dilated(x, w[:, None, :].astype(x.dtype), window_strides=(1,),
                                 padding=[(k - 1, 0)], dimension_numbers=('NWC', 'WIO', 'NWC'),
                                 feature_group_count=c)
    return y + b.astype(x.dtype)


def forgetting_attention(q, k, v, log_f):
    B, S, H, Dh = q.shape
    nb = S // Q_BLOCK
    c = jnp.cumsum(log_f, axis=1)
    c_k = jnp.transpose(c, (0, 2, 1))[:, :, None, :]
    qb = q.reshape(B, nb, Q_BLOCK, H, Dh).swapaxes(0, 1)
    cb = c.reshape(B, nb, Q_BLOCK, H).swapaxes(0, 1)
    kpos = jnp.arange(S)
    scale = Dh ** -0.5

    def block(args):
        qi, ci, i = args
        s = jnp.einsum('bqhd,bkhd->bhqk', qi, k, preferred_element_type=jnp.float32) * scale
        s = s + jnp.transpose(ci, (0, 2, 1))[..., None] - c_k
        qpos = i * Q_BLOCK + jnp.arange(Q_BLOCK)
        s = jnp.where(kpos[None, :] <= qpos[:, None], s, -jnp.inf)
        p = jax.nn.softmax(s, axis=-1)
        return jnp.einsum('bhqk,bkhd->bqhd', p.astype(v.dtype), v)

    o = lax.map(block, (qb, cb, jnp.arange(nb)))
    return o.swapaxes(0, 1).reshape(B, S, H, Dh)


def dilated_branch(q, k, v, window, dilation):
    B, S, H, Dh = q.shape
    blk = window // dilation
    span = blk * dilation
    sp = -(-S // span) * span
    nb = sp // span
    pad = ((0, 0), (0, sp - S), (0, 0), (0, 0))

    def strided(t):
        return jnp.pad(t, pad).reshape(B, nb, blk, dilation, H, Dh)

    qs, ks, vs = strided(q), strided(k), strided(v)
    def with_prev(t):
        prev = jnp.concatenate([jnp.zeros_like(t[:, :1]), t[:, :-1]], axis=1)
        return jnp.concatenate([prev, t], axis=2)
    kc, vc = with_prev(ks), with_prev(vs)
    s = jnp.einsum('bnqrhd,bnkrhd->bnrhqk', qs, kc, preferred_element_type=jnp.float32) * (Dh ** -0.5)
    qi = blk + jnp.arange(blk)
    ki = jnp.arange(2 * blk)
    delta = qi[:, None] - ki[None, :]
    band = (delta >= 0) & (delta <= blk)
    first = (jnp.arange(nb) == 0)[:, None, None] & (ki < blk)[None, None, :]
    valid = band[None] & ~first
    s = jnp.where(valid[None, :, None, None], s, -jnp.inf)
    m = jnp.max(s, axis=-1, keepdims=True)
    e = jnp.exp(s - m)
    l = jnp.sum(e, axis=-1)
    num = jnp.einsum('bnrhqk,bnkrhd->bnqrhd', e.astype(v.dtype), vc,
                     preferred_element_type=jnp.float32)
    num = num.reshape(B, sp, H, Dh)[:, :S]
    def to_seq(t):
        return jnp.transpose(t, (0, 1, 4, 2, 3)).reshape(B, sp, H)[:, :S]
    return num, to_seq(m[..., 0]), to_seq(l)


def dilated_mixture(q, k, v):
    parts = [dilated_branch(q, k, v, w, d) for (w, d) in DILATION_PAIRS]
    m_all = parts[0][1]
    for _, m, _ in parts[1:]:
        m_all = jnp.maximum(m_all, m)
    num = 0.0
    den = 0.0
    for n_p, m_p, l_p in parts:
        a = jnp.exp(m_p - m_all)
        num = num + a[..., None] * n_p
        den = den + a * l_p
    return (num / den[..., None]).astype(q.dtype)


def setup_inputs(seed: int = 0) -> dict:
    key = jax.random.key(seed)
    ks = jax.random.split(key, 20)
    f32 = jnp.float32
    def nrm(k, shape, scale):
        return jax.random.normal(k, shape, f32) * scale
    return {
        'x': nrm(ks[0], (BATCH, SEQ, D_MODEL), 1.0),
        'ln1_g': 1.0 + nrm(ks[1], (DEPTH, D_MODEL), 0.02),
        'w_in': nrm(ks[2], (DEPTH, D_MODEL, N_IN), D_MODEL ** -0.5),
        'b_forget': jax.random.uniform(ks[3], (DEPTH, N_HEADS_FOX), f32, 1.0, 4.0),
        'g_out_fox': 1.0 + nrm(ks[4], (DEPTH, W_FOX), 0.02),
        'g_out_dil': 1.0 + nrm(ks[5], (DEPTH, W_DIL), 0.02),
        'conv_w': nrm(ks[6], (DEPTH, CONV_K, CONV_CH), CONV_K ** -0.5),
        'conv_b': nrm(ks[7], (DEPTH, CONV_CH), 0.02),
        'cnorm_g': 1.0 + nrm(ks[8], (DEPTH, CONV_CH), 0.02),
        'cnorm_b': nrm(ks[9], (DEPTH, CONV_CH), 0.02),
        'w_o': nrm(ks[10], (DEPTH, D_MODEL, D_MODEL), D_MODEL ** -0.5),
        'ln2_g': 1.0 + nrm(ks[11], (DEPTH, D_MODEL), 0.02),
        'w_up': nrm(ks[12], (DEPTH, D_MODEL, 2 * D_FF), D_MODEL ** -0.5),
        'ffn_conv_w': nrm(ks[13], (DEPTH, FFN_CONV_K, 2 * D_FF), FFN_CONV_K ** -0.5),
        'ffn_conv_b': nrm(ks[14], (DEPTH, 2 * D_FF), 0.02),
        'w_down': nrm(ks[15], (DEPTH, D_FF, D_MODEL), D_FF ** -0.5),
        'g_final': 1.0 + nrm(ks[16], (D_MODEL,), 0.02),
    }


def reference(x, ln1_g, w_in, b_forget, g_out_fox, g_out_dil, conv_w, conv_b, cnorm_g, cnorm_b,
              w_o, ln2_g, w_up, ffn_conv_w, ffn_conv_b, w_down, g_final):
    B, S, _ = x.shape
    for l in range(DEPTH):
        h = rmsnorm(x, ln1_g[l])
        p = jnp.einsum('bsd,dn->bsn', h, w_in[l])
        qa = p[..., OFF_QA:OFF_KA].reshape(B, S, N_HEADS_FOX, HEAD_DIM)
        ka = p[..., OFF_KA:OFF_VA].reshape(B, S, N_HEADS_FOX, HEAD_DIM)
        va = p[..., OFF_VA:OFF_FA].reshape(B, S, N_HEADS_FOX, HEAD_DIM)
        log_f = jax.nn.log_sigmoid(p[..., OFF_FA:OFF_QB].astype(jnp.float32)
                                   + b_forget[l].astype(jnp.float32))
        ya = forgetting_attention(qa, ka, va, log_f).reshape(B, S, W_FOX)
        ya = rmsnorm(ya, g_out_fox[l])
        qb = p[..., OFF_QB:OFF_KB].reshape(B, S, N_HEADS_DIL, HEAD_DIM)
        kb = p[..., OFF_KB:OFF_VB].reshape(B, S, N_HEADS_DIL, HEAD_DIM)
        vb = p[..., OFF_VB:OFF_GV].reshape(B, S, N_HEADS_DIL, HEAD_DIM)
        yb = dilated_mixture(qb, kb, vb).reshape(B, S, W_DIL)
        yb = rmsnorm(yb, g_out_dil[l])
        yc = p[..., OFF_GV:OFF_GG] * jax.nn.sigmoid(p[..., OFF_GG:N_IN])
        yc = causal_dwconv(yc, conv_w[l], conv_b[l])
        yc = jax.nn.silu(layernorm(yc, cnorm_g[l], cnorm_b[l]))
        y = jnp.concatenate([ya, yb, yc], axis=-1)
        x = x + jnp.einsum('bsd,de->bse', y, w_o[l])
        h2 = rmsnorm(x, ln2_g[l])
        u = jnp.einsum('bsd,df->bsf', h2, w_up[l])
        u = causal_dwconv(u, ffn_conv_w[l], ffn_conv_b[l])
        hidden = jax.nn.silu(u[..., :D_FF]) * u[..., D_FF:]
        x = x + jnp.einsum('bsf,fd->bsd', hidden, w_down[l])
    return rmsnorm(x, g_final)
```

```python
import contextlib
import numpy as np
import concourse.bass as bass
import concourse.mybir as mybir
from concourse.bass_utils import run_bass_kernel_spmd

F32 = mybir.dt.float32
BF16 = mybir.dt.bfloat16
AF = mybir.ActivationFunctionType
ALU = mybir.AluOpType

D_MODEL = 1024
T = 2048
TC = 512
NTC = 4
KD = 8
HD = 64
NH = 6
W_FOX = 384
W_DIL = 384
CONV_CH = 256
CONV_K = 31
D_FF = 2816
NFF = 22
EPS = 1e-6
OFF_QA = 0
OFF_KA = 384
OFF_VA = 768
OFF_FA = 1152
OFF_QB = 1158
OFF_KB = OFF_QB + 384
OFF_VB = OFF_KB + 384
OFF_GV = OFF_VB + 384
OFF_GG = OFF_GV + 256
N_IN = OFF_GG + 256
TTW = 2432
N_CORES = 8

ENGINES = ("pe", "act", "dve", "pool", "sp")


class Op:
    __slots__ = ("eng", "fn", "dma", "deps", "sig", "needed", "gid")

    def __init__(self, eng, fn, dma, gid):
        self.eng = eng
        self.fn = fn
        self.dma = dma
        self.deps = []
        self.sig = None
        self.needed = False
        self.gid = gid


class Sched:
    def __init__(self, nc, n_dma_sems=32):
        self.nc = nc
        self.ops = {e: [] for e in ENGINES}
        self.tok = {}
        self.gid = 0
        self.n_dma_sems = n_dma_sems
        self.all_dma = []

    def op(self, eng, fn, reads=(), writes=(), dma=False, extra_deps=()):
        o = Op(eng, fn, dma, self.gid)
        self.gid += 1
        deps = {}
        for t in reads:
            st = self.tok.get(t)
            if st is not None and st[0] is not None:
                deps[id(st[0])] = st[0]
        for t in writes:
            st = self.tok.get(t)
            if st is not None:
                if st[0] is not None:
                    deps[id(st[0])] = st[0]
                for r in st[1]:
                    deps[id(r)] = r
        for d in extra_deps:
            deps[id(d)] = d
        for t in reads:
            st = self.tok.get(t)
            if st is None:
                st = [None, []]
                self.tok[t] = st
            st[1].append(o)
        for t in writes:
            self.tok[t] = [o, []]
        dl = []
        for d in deps.values():
            if d is o:
                continue
            if eng == "pe" and d.eng == "pe" and not d.dma and not dma:
                continue
            dl.append(d)
        o.deps = dl
        for d in dl:
            d.needed = True
        self.ops[eng].append(o)
        if dma:
            self.all_dma.append(o)
        return o

    def barrier(self):
        lasts = []
        for e in ENGINES:
            for o in reversed(self.ops[e]):
                if not o.dma and o.fn is not None:
                    lasts.append(o)
                    break
        dmas = {}
        for st in self.tok.values():
            if st[0] is not None and st[0].dma:
                dmas[id(st[0])] = st[0]
            for r in st[1]:
                if r.dma:
                    dmas[id(r)] = r
        deps = lasts + list(dmas.values())
        for e in ENGINES:
            if e == "pe":
                self.op(e, None, extra_deps=[d for d in deps])
            else:
                self.op(e, None, extra_deps=deps)

    def emit(self):
        nc = self.nc
        eng_attr = {"pe": "tensor", "act": "scalar", "dve": "vector", "pool": "gpsimd", "sp": "sync"}
        with contextlib.ExitStack() as es:
            esem = {e: es.enter_context(nc.semaphore("s_" + e)) for e in ENGINES}
            dsem = [es.enter_context(nc.semaphore("d%d" % i)) for i in range(self.n_dma_sems)]
            last_on = [None] * self.n_dma_sems
            cnt = [0] * self.n_dma_sems
            for k, o in enumerate(sorted(self.all_dma, key=lambda x: x.gid)):
                s = k % self.n_dma_sems
                cnt[s] += 16
                o.sig = (dsem[s], cnt[s])
                if last_on[s] is not None:
                    o.deps.append(last_on[s])
                last_on[s] = o
            for e in ENGINES:
                c = 0
                for o in self.ops[e]:
                    if o.dma:
                        continue
                    if o.needed:
                        assert o.fn is not None
                        c += 1
                        o.sig = (esem[e], c)
            block = es.enter_context(nc.Block())
            for e in ENGINES:
                ops = self.ops[e]
                if not ops:
                    continue

                def body(eng, ops=ops):
                    waited = {}
                    for o in ops:
                        for d in o.deps:
                            s, v = d.sig
                            if waited.get(id(s), 0) < v:
                                eng.wait_ge(s, v)
                                waited[id(s)] = v
                        if o.fn is None:
                            continue
                        ins = o.fn(eng)
                        if o.dma:
                            ins.then_inc(o.sig[0], 16)
                        elif o.sig is not None:
                            ins.then_inc(o.sig[0], 1)

                getattr(block, eng_attr[e])(body)


class Arena:
    def __init__(self, nc, limit):
        self.nc = nc
        self.off = 16512
        self.limit = limit
        self.n = 0

    def alloc(self, name, shape, dtype):
        esz = 4 if dtype == F32 else 2
        nb = esz
        for s in shape[1:]:
            nb *= s
        nb = (nb + 63) // 64 * 64
        off = self.off
        self.off += nb
        assert self.off <= self.limit, "SBUF arena overflow at %s: %d > %d" % (name, self.off, self.limit)
        self.n += 1
        return self.nc.alloc_sbuf_tensor_at("%s_%d" % (name, self.n), list(shape), dtype, offset=off)

    def mark(self):
        return self.off

    def reset(self, m):
        self.off = m


def build_program(n_seq, n_layers, do_mixer=True, do_ffn=True, do_final=True):
    nc = bass.Bass("TRN2", target_bir_lowering=False)

    def din(name, shape):
        return nc.dram_tensor(name, list(shape), F32, kind="ExternalInput").ap()

    xT_d = din("xT", [n_seq, KD, 128, T])
    yT_d = nc.dram_tensor("yT", [n_seq, KD, 128, T], F32, kind="ExternalOutput").ap()
    c_ident = din("c_ident", [128, 128])
    c_ones = din("c_ones", [128, 128])
    c_tri = din("c_tri", [128, 128])
    c_ttd = din("c_ttd", [128, TTW])
    gfin_d = din("g_final", [128, KD])
    WL = []
    for l in range(n_layers):
        w = {}
        w["wqkf"] = din("wqkf%d" % l, [NH, 128, KD, 128])
        w["wqkd"] = din("wqkd%d" % l, [3, 2, 128, KD, 128])
        w["wv"] = din("wv%d" % l, [2, 128, KD, 384])
        w["wfa"] = din("wfa%d" % l, [128, KD, NH])
        w["wconv"] = din("wconv%d" % l, [128, KD, 512])
        w["wo"] = din("wo%d" % l, [128, KD, D_MODEL])
        w["wup"] = din("wup%d" % l, [NFF, 128, KD, 256])
        w["wdown"] = din("wdown%d" % l, [128, NFF, D_MODEL])
        w["ln1"] = din("ln1_%d" % l, [128, KD])
        w["ln2"] = din("ln2_%d" % l, [128, KD])
        w["gfox"] = din("gfox%d" % l, [128, 3])
        w["gdil"] = din("gdil%d" % l, [128, 3])
        w["convw"] = din("convw%d" % l, [128, 2, CONV_K])
        w["convb"] = din("convb%d" % l, [128, 2])
        w["cng"] = din("cng%d" % l, [128, 2])
        w["cnb"] = din("cnb%d" % l, [128, 2])
        w["ffnw"] = din("ffnw%d" % l, [128, 2 * NFF, 3])
        w["ffnb"] = din("ffnb%d" % l, [128, 2 * NFF])
        w["bf"] = din("bf%d" % l, [NH, 1])
        WL.append(w)

    S = Sched(nc)
    A = Arena(nc, 229344)
    ps = nc.alloc_psum_tensor("ps", [128, 4096], F32)

    def bank(b, p0=0, p1=128, c0=0, c1=512):
        return ps[p0:p1, b * 512 + c0:b * 512 + c1]

    def PSK(b):
        return ("ps", b)

    xT = A.alloc("xT", [128, KD, T], F32)
    hT = A.alloc("hT", [128, KD, T], BF16)
    ident_f = A.alloc("ident_f", [128, 128], F32)
    ones_f = A.alloc("ones_f", [128, 128], F32)
    ident_b = A.alloc("ident_b", [128, 128], BF16)
    ones_b = A.alloc("ones_b", [128, 128], BF16)
    tri_b = A.alloc("tri_b", [128, 128], BF16)
    ttd_b = A.alloc("ttd_b", [128, TTW], BF16)
    ones512 = A.alloc("ones512", [128, 512], F32)
    gfin = A.alloc("gfin", [128, KD], F32)
    PL = []
    for l in range(n_layers):
        p = {}
        p["ln1"] = A.alloc("ln1", [128, KD], F32)
        p["ln2"] = A.alloc("ln2", [128, KD], F32)
        p["gfox"] = A.alloc("gfox", [128, 3], F32)
        p["gdil"] = A.alloc("gdil", [128, 3], F32)
        p["convw"] = A.alloc("convw", [128, 2, CONV_K], F32)
        p["convb"] = A.alloc("convb", [128, 2], F32)
        p["cng"] = A.alloc("cng", [128, 2], F32)
        p["cnb"] = A.alloc("cnb", [128, 2], F32)
        p["ffnw"] = A.alloc("ffnw", [128, 2 * NFF, 3], F32)
        p["ffnb"] = A.alloc("ffnb", [128, 2 * NFF], F32)
        p["bf"] = A.alloc("bf", [NH, 1], F32)
        p["negb"] = A.alloc("negb", [NH, 1], F32)
        PL.append(p)
    sqring = [A.alloc("sq", [128, TC], BF16) for _ in range(3)]
    rsring = [A.alloc("rs", [128, TC], F32) for _ in range(2)]
    ctr = {"sq": 0, "rs": 0}

    def ld(eng, dst_ap, src_ap, tokw):
        return S.op(eng, lambda e: e.dma_start(out=dst_ap, in_=src_ap), writes=tokw, dma=True)

    ld("sp", ident_f[:], c_ident, ["c_ident_f"])
    ld("sp", ones_f[:], c_ones, ["c_ones_f"])
    ld("pool", ident_b[:], c_ident, ["c_ident_b"])
    ld("pool", ones_b[:], c_ones, ["c_ones_b"])
    ld("pool", tri_b[:], c_tri, ["c_tri"])
    ld("pool", ttd_b[:], c_ttd, ["c_ttd"])
    ld("sp", gfin[:], gfin_d, ["gfin"])
    S.op("dve", lambda e: e.memset(ones512[:], 1.0), writes=["ones512"])
    for l in range(n_layers):
        for k in ("ln1", "ln2", "gfox", "gdil", "convw", "convb", "cng", "cnb", "ffnw", "ffnb", "bf"):
            ld("sp", PL[l][k][:], WL[l][k], [("par", l, k)])
        S.op("dve", lambda e, l=l: e.tensor_scalar(out=PL[l]["negb"][:], in0=PL[l]["bf"][:], scalar1=-1.0,
                                                  scalar2=None, op0=ALU.mult),
             reads=[("par", l, "bf")], writes=[("par", l, "negb")])

    scope_base = A.mark()

    def tsl(t):
        return slice(t * TC, (t + 1) * TC)

    def rmsnorm_to_h(g_tile, gtok, banks):
        for t in range(NTC):
            b = banks[t % len(banks)]
            for c in range(KD):
                i = ctr["sq"] % len(sqring)
                ctr["sq"] += 1
                sq = sqring[i]
                S.op("act", lambda e, sq=sq, c=c, t=t: e.activation(out=sq[:], in_=xT[:, c, tsl(t)], func=AF.Square),
                     reads=[("x", c, t)], writes=[("sq", i)])
                S.op("pe", lambda e, sq=sq, c=c, b=b: e.matmul(bank(b), lhsT=ones_b[:], rhs=sq[:], start=(c == 0),
                                                               stop=(c == KD - 1)),
                     reads=[("sq", i), "c_ones_b"], writes=[PSK(b)])
            r = ctr["rs"] % len(rsring)
            ctr["rs"] += 1
            rs = rsring[r]
            S.op("act", lambda e, rs=rs, b=b: e.activation(out=rs[:], in_=bank(b), func=AF.Ln, scale=1.0 / D_MODEL, bias=EPS),
                 reads=[PSK(b)], writes=[("rs", r)])
            S.op("act", lambda e, rs=rs: e.activation(out=rs[:], in_=rs[:], func=AF.Exp, scale=-0.5),
                 reads=[("rs", r)], writes=[("rs", r)])
            for c in range(KD):
                S.op("dve", lambda e, rs=rs, c=c, t=t: e.scalar_tensor_tensor(out=hT[:, c, tsl(t)], in0=xT[:, c, tsl(t)],
                                                                           scalar=g_tile[:, c:c + 1], in1=rs[:],
                                                                           op0=ALU.mult, op1=ALU.mult),
                     reads=[("x", c, t), ("rs", r), gtok], writes=[("h", c, t)])

    def proj(wfn, wtoks, M, t, b, ncols=TC):
        for k in range(KD):
            S.op("pe", lambda e, k=k: e.matmul(bank(b, 0, M), lhsT=wfn(k), rhs=hT[:, k, tsl(t)], start=(k == 0),
                                               stop=(k == KD - 1)),
                 reads=[("h", k, t)] + wtoks, writes=[PSK(b)])

    def group_norm_wo(l, yTg, npc, gtile, gtok, width, wo_chunk0, banks, wo_t, t_outer=False):
        if gtile is not None:
            for t in range(NTC):
                b = banks[t % len(banks)]
                for pc in range(npc):
                    i = ctr["sq"] % len(sqring)
                    ctr["sq"] += 1
                    sq = sqring[i]
                    S.op("act", lambda e, sq=sq, pc=pc, t=t: e.activation(out=sq[:], in_=yTg[:, pc, tsl(t)], func=AF.Square),
                         reads=[("yg", pc, t, 0), ("yg", pc, t, 1)], writes=[("sq", i)])
                    S.op("pe", lambda e, sq=sq, pc=pc, b=b: e.matmul(bank(b), lhsT=ones_b[:], rhs=sq[:], start=(pc == 0),
                                                                     stop=(pc == npc - 1)),
                         reads=[("sq", i), "c_ones_b"], writes=[PSK(b)])
                r = ctr["rs"] % len(rsring)
                ctr["rs"] += 1
                rs = rsring[r]
                S.op("act", lambda e, rs=rs, b=b: e.activation(out=rs[:], in_=bank(b), func=AF.Ln, scale=1.0 / width, bias=EPS),
                     reads=[PSK(b)], writes=[("rs", r)])
                S.op("act", lambda e, rs=rs: e.activation(out=rs[:], in_=rs[:], func=AF.Exp, scale=-0.5),
                     reads=[("rs", r)], writes=[("rs", r)])
                for pc in range(npc):
                    S.op("dve", lambda e, rs=rs, pc=pc, t=t: e.scalar_tensor_tensor(
                        out=yTg[:, pc, tsl(t)], in0=yTg[:, pc, tsl(t)], scalar=gtile[:, pc:pc + 1], in1=rs[:],
                        op0=ALU.mult, op1=ALU.mult),
                         reads=[("yg", pc, t, 0), ("yg", pc, t, 1), ("rs", r), gtok],
                         writes=[("yg", pc, t, 0), ("yg", pc, t, 1)])
        ld("pool", wo_t[:, 0:npc, :], WL[l]["wo"][:, wo_chunk0:wo_chunk0 + npc, :], ["wo"])
        n = 0
        order = [(e_, t) for t in range(NTC) for e_ in range(KD)] if t_outer else [(e_, t) for e_ in range(KD) for t in range(NTC)]
        for e_, t in order:
            b = banks[n % len(banks)]
            n += 1
            for pc in range(npc):
                S.op("pe", lambda e, pc=pc, e_=e_, t=t, b=b: e.matmul(bank(b), lhsT=wo_t[:, pc, e_ * 128:(e_ + 1) * 128],
                                                                   rhs=yTg[:, pc, tsl(t)], start=(pc == 0),
                                                                   stop=(pc == npc - 1)),
                     reads=[("yg", pc, t, 0), ("yg", pc, t, 1), "wo"], writes=[PSK(b)])
            S.op("dve", lambda e, e_=e_, t=t, b=b: e.tensor_tensor(out=xT[:, e_, tsl(t)], in0=bank(b), in1=xT[:, e_, tsl(t)],
                                                               op=ALU.add),
                 reads=[PSK(b), ("x", e_, t)], writes=[("x", e_, t)])

    def conv_branch(l):
        A.reset(scope_base)
        P = PL[l]
        wconv = A.alloc("wconv", [128, KD, 512], BF16)
        ycpad = A.alloc("ycpad", [128, 2, 30 + T], BF16)
        dg = A.alloc("dg", [128, 2, CONV_K, 128], BF16)
        yTg = A.alloc("yTgc", [128, 2, T], BF16)
        wo_t = A.alloc("wo_c", [128, 2, D_MODEL], BF16)
        sg = [A.alloc("sg", [128, TC], F32) for _ in range(2)]
        u = [[A.alloc("u", [128, TC], F32) for _ in range(2)] for _ in range(2)]
        dd = [[A.alloc("dd", [128, TC], F32) for _ in range(2)] for _ in range(2)]
        sq2 = [[A.alloc("sq2", [128, TC], F32) for _ in range(2)] for _ in range(2)]
        ld("pool", wconv[:], WL[l]["wconv"], ["wconv"])
        S.op("dve", lambda e: e.memset(ycpad[:, :, 0:30], 0.0), writes=[("ycpad", 0), ("ycpad", 1)])
        for cc in range(2):
            for k in range(CONV_K):
                S.op("dve", lambda e, cc=cc, k=k: e.tensor_scalar(out=dg[:, cc, k, :], in0=ident_b[:],
                                                                 scalar1=P["convw"][:, cc, k:k + 1], scalar2=None,
                                                                 op0=ALU.mult),
                     reads=["c_ident_b", ("par", l, "convw")], writes=[("dg", cc, k)])
        pb = [0, 1, 2, 3]
        n = 0
        for cc in range(2):
            for t in range(NTC):
                ba = pb[n % 4]
                bb = pb[(n + 1) % 4]
                n += 2
                proj(lambda k, cc=cc: wconv[:, k, cc * 128:(cc + 1) * 128], ["wconv"], 128, t, ba)
                proj(lambda k, cc=cc: wconv[:, k, 256 + cc * 128:256 + (cc + 1) * 128], ["wconv"], 128, t, bb)
                si = (n // 2) % 2
                S.op("act", lambda e, si=si, bb=bb: e.activation(out=sg[si][:], in_=bank(bb), func=AF.Sigmoid),
                     reads=[PSK(bb)], writes=[("sg", si)])
                S.op("dve", lambda e, si=si, ba=ba, cc=cc, t=t: e.tensor_tensor(
                    out=ycpad[:, cc, 30 + t * TC:30 + (t + 1) * TC], in0=bank(ba), in1=sg[si][:], op=ALU.mult),
                     reads=[PSK(ba), ("sg", si)], writes=[("yc", cc, t)])
        for t in range(NTC):
            r2 = t % 2
            for cc in range(2):
                b = pb[(2 * t + cc) % 4]
                for k in range(CONV_K):
                    rd = [("dg", cc, k), ("yc", cc, t), ("ycpad", cc)]
                    if t > 0:
                        rd.append(("yc", cc, t - 1))
                    S.op("pe", lambda e, cc=cc, k=k, t=t, b=b: e.matmul(bank(b), lhsT=dg[:, cc, k, :],
                                                                     rhs=ycpad[:, cc, t * TC + k:t * TC + k + TC],
                                                                     start=(k == 0), stop=(k == CONV_K - 1)),
                         reads=rd, writes=[PSK(b)])
                S.op("act", lambda e, cc=cc, b=b, r2=r2: e.activation(out=u[cc][r2][:], in_=bank(b), func=AF.Identity,
                                                                   bias=P["convb"][:, cc:cc + 1], scale=1.0),
                     reads=[PSK(b), ("par", l, "convb")], writes=[("u", cc, r2)])
            bs = 4 + (t % 2) * 2
            for cc in range(2):
                S.op("pe", lambda e, cc=cc, r2=r2, bs=bs: e.matmul(bank(bs), lhsT=ones_f[:], rhs=u[cc][r2][:], start=(cc == 0),
                                                                stop=(cc == 1)),
                     reads=[("u", cc, r2), "c_ones_f"], writes=[PSK(bs)])
            for cc in range(2):
                S.op("dve", lambda e, cc=cc, r2=r2, bs=bs: e.scalar_tensor_tensor(
                    out=dd[cc][r2][:], in0=bank(bs), scalar=-1.0 / CONV_CH, in1=u[cc][r2][:], op0=ALU.mult, op1=ALU.add),
                     reads=[PSK(bs), ("u", cc, r2)], writes=[("dd", cc, r2)])
                S.op("act", lambda e, cc=cc, r2=r2: e.activation(out=sq2[cc][r2][:], in_=dd[cc][r2][:], func=AF.Square),
                     reads=[("dd", cc, r2)], writes=[("sq2", cc, r2)])
            for cc in range(2):
                S.op("pe", lambda e, cc=cc, r2=r2, bs=bs: e.matmul(bank(bs + 1), lhsT=ones_f[:], rhs=sq2[cc][r2][:],
                                                                start=(cc == 0), stop=(cc == 1)),
                     reads=[("sq2", cc, r2), "c_ones_f"], writes=[PSK(bs + 1)])
            r = ctr["rs"] % len(rsring)
            ctr["rs"] += 1
            rs = rsring[r]
            S.op("act", lambda e, rs=rs, bs=bs: e.activation(out=rs[:], in_=bank(bs + 1), func=AF.Ln, scale=1.0 / CONV_CH, bias=EPS),
                 reads=[PSK(bs + 1)], writes=[("rs", r)])
            S.op("act", lambda e, rs=rs: e.activation(out=rs[:], in_=rs[:], func=AF.Exp, scale=-0.5),
                 reads=[("rs", r)], writes=[("rs", r)])
            for cc in range(2):
                S.op("dve", lambda e, cc=cc, r2=r2, rs=rs: e.tensor_tensor(out=dd[cc][r2][:], in0=dd[cc][r2][:], in1=rs[:],
                                                                        op=ALU.mult),
                     reads=[("dd", cc, r2), ("rs", r)], writes=[("dd", cc, r2)])
                S.op("act", lambda e, cc=cc, r2=r2, t=t: e.activation(out=yTg[:, cc, tsl(t)], in_=dd[cc][r2][:], func=AF.Silu,
                                                                   scale=P["cng"][:, cc:cc + 1], bias=P["cnb"][:, cc:cc + 1]),
                     reads=[("dd", cc, r2), ("par", l, "cng"), ("par", l, "cnb")],
                     writes=[("yg", cc, t, 0), ("yg", cc, t, 1)])
        group_norm_wo(l, yTg, 2, None, None, None, 6, [0, 1, 2, 3], wo_t)
        S.barrier()

    def attn_group(l, kind, post=None):
        A.reset(scope_base)
        P = PL[l]
        fox = kind == "fox"
        KQ = 67
        yTg = A.alloc("yTg", [128, 3, T], BF16)
        m_wo = A.mark()
        wo_t = A.alloc("wo_a", [128, 3, D_MODEL], BF16)
        A.reset(m_wo)
        if fox:
            qaug = [A.alloc("qaug", [128, T], BF16) for _ in range(2)]
            kaug = [A.alloc("kaug", [128, T], BF16) for _ in range(2)]
            wqk = [A.alloc("wqk", [128, KD, 128], BF16) for _ in range(2)]
            kst = [A.alloc("kst", [128, TC], BF16) for _ in range(2)]
        else:
            qaug = [A.alloc("qpair", [128, T], BF16) for _ in range(2)]
            kpad = [[A.alloc("kpad", [128, T], BF16) for _ in range(2)] for _ in range(2)]
            wqk = [[A.alloc("wqk", [128, KD, 128], BF16) for _ in range(2)] for _ in range(2)]
        wv = A.alloc("wv", [128, KD, 384], BF16)
        NPT = 6
        pT = [A.alloc("pT", [128, TC], BF16) for _ in range(NPT)]
        numsb = [A.alloc("numsb", [128, TC], F32) for _ in range(2)]
        bc = [A.alloc("bc", [128, TC], F32) for _ in range(2)]
        if fox:
            dkT = A.alloc("dkT", [128, 16 * NH], F32)
            csplit = A.alloc("csplit", [NH, 3, T], BF16)
            m2 = A.mark()
            wfa = A.alloc("wfa", [128, KD, NH], BF16)
            dT = A.alloc("dT", [NH, T], F32)
            e6 = [A.alloc("e6", [NH, TC], F32) for _ in range(2)]
            tA = A.alloc("tA", [NH, TC], F32)
            tB = A.alloc("tB", [NH, TC], F32)
            A.reset(m2)
        vaug = A.alloc("vaug", [128, 16, NH, 128], BF16)
        PB = [6, 7]
        pbn = [0]

        def nextpb():
            b = PB[pbn[0] % 2]
            pbn[0] += 1
            return b

        grp = 0 if fox else 1
        ld("pool", wv[:], WL[l]["wv"][grp], ["wv"])
        for s_ in range(2):
            if fox:
                S.op("dve", lambda e, s_=s_: e.memset(kaug[s_][64:67, :], 1.0), writes=[("kones", s_)])
            else:
                S.op("dve", lambda e, s_=s_: e.memset(kpad[s_][0][64:128, :], 0.0), writes=[("kz", s_, 0)])
                S.op("dve", lambda e, s_=s_: e.memset(kpad[s_][1][0:64, :], 0.0), writes=[("kz", s_, 1)])
        if fox:
            ld("pool", wfa[:], WL[l]["wfa"], ["wfa"])
            for t in range(NTC):
                b = nextpb()
                proj(lambda k: wfa[:, k, :], ["wfa"], NH, t, b)
                ei = t % 2
                S.op("act", lambda e, b=b, ei=ei: e.activation(out=e6[ei][:], in_=bank(b, 0, NH), func=AF.Exp, scale=-1.0,
                                                            bias=P["negb"][:, 0:1]),
                     reads=[PSK(b), ("par", l, "negb")], writes=[("e6", ei)])
                S.op("act", lambda e, ei=ei: e.activation(out=e6[ei][:], in_=e6[ei][:], func=AF.Ln, scale=1.0, bias=1.0),
                     reads=[("e6", ei)], writes=[("e6", ei)])
                init = 0.0 if t == 0 else dT[:, t * TC - 1:t * TC]
                rd = [("e6", ei), "ones512"] + ([("dT", t - 1)] if t > 0 else [])
                S.op("dve", lambda e, ei=ei, t=t, init=init: e.tensor_tensor_scan(
                    out=dT[:, tsl(t)], data0=ones512[0:NH, :], data1=e6[ei][:], initial=init, op0=ALU.mult, op1=ALU.add),
                     reads=rd, writes=[("dT", t)])
                S.op("dve", lambda e, t=t: e.tensor_scalar(out=tA[:], in0=dT[:, tsl(t)], scalar1=-8.0, scalar2=None, op0=ALU.mult),
                     reads=[("dT", t)], writes=["tA"])
                S.op("dve", lambda e, t=t: e.tensor_copy(out=csplit[:, 0, tsl(t)], in_=tA[:]), reads=["tA"], writes=[("cs", 0, t)])
                S.op("dve", lambda e, t=t: e.tensor_tensor(out=tB[:], in0=tA[:], in1=csplit[:, 0, tsl(t)], op=ALU.subtract),
                     reads=["tA", ("cs", 0, t)], writes=["tB"])
                S.op("dve", lambda e, t=t: e.tensor_copy(out=csplit[:, 1, tsl(t)], in_=tB[:]), reads=["tB"], writes=[("cs", 1, t)])
                S.op("dve", lambda e, t=t: e.tensor_tensor(out=tA[:], in0=tB[:], in1=csplit[:, 1, tsl(t)], op=ALU.subtract),
                     reads=["tB", ("cs", 1, t)], writes=["tA"])
                S.op("dve", lambda e, t=t: e.tensor_copy(out=csplit[:, 2, tsl(t)], in_=tA[:]), reads=["tA"], writes=[("cs", 2, t)])
            bt = nextpb()
            for j in range(16):
                S.op("pe", lambda e, j=j, bt=bt: e.transpose(out=bank(bt, 0, 128, j * NH, (j + 1) * NH),
                                                            in_=dT[:, j * 128:(j + 1) * 128], identity=ident_f[0:NH, 0:NH]),
                     reads=[("dT", j // 4), "c_ident_f"], writes=[PSK(bt)])
            S.op("dve", lambda e, bt=bt: e.tensor_copy(out=dkT[:], in_=bank(bt, 0, 128, 0, 16 * NH)),
                 reads=[PSK(bt)], writes=["dkT"])
            S.barrier()
        S.op("dve", lambda e: e.memset(vaug[:, :, 0:NH:2, 64:128], 1.0), writes=["vones"])
        S.op("dve", lambda e: e.memset(vaug[:, :, 1:NH:2, 0:64], 1.0), writes=["vones2"])
        for j in range(16):
            b = nextpb()
            for k in range(KD):
                S.op("pe", lambda e, j=j, k=k, b=b: e.matmul(bank(b, 0, 128, 0, 384), lhsT=hT[:, k, j * 128:(j + 1) * 128],
                                                           rhs=wv[:, k, :], start=(k == 0), stop=(k == KD - 1)),
                     reads=[("h", k, j // 4), "wv"], writes=[PSK(b)])
            S.op("act", lambda e, j=j, b=b: e.activation(
                out=vaug[:, j, 0:NH:2, 0:64],
                in_=bank(b, 0, 128, 0, 384).rearrange("p (e two d) -> p e two d", two=2, d=HD)[:, :, 0, :], func=AF.Copy),
                 reads=[PSK(b)], writes=[("v", j, 0)])
            S.op("act", lambda e, j=j, b=b: e.activation(
                out=vaug[:, j, 1:NH:2, 64:128],
                in_=bank(b, 0, 128, 0, 384).rearrange("p (e two d) -> p e two d", two=2, d=HD)[:, :, 1, :], func=AF.Copy),
                 reads=[PSK(b)], writes=[("v", j, 1)])

        wsrc = WL[l]["wqkf"] if fox else WL[l]["wqkd"]
        SB = [0, 1, 2, 3]
        OB = [4, 5]
        nblk = [0]
        nnorm = [0]
        nkst = [0]
        pending = []
        units = [[h] for h in range(NH)] if fox else [[0, 1], [2, 3], [4, 5]]

        def load_unit(u):
            s = u % 2
            if fox:
                ld("pool", wqk[s][:], wsrc[u], [("wqk", s)])
            else:
                ld("pool", wqk[s][0][:], wsrc[u, 0], [("wqk", s, 0)])
                ld("pool", wqk[s][1][:], wsrc[u, 1], [("wqk", s, 1)])

        def make_proj_items(u):
            s = u % 2
            items = []
            for t in range(NTC):
                for which in ((0,) if fox else (1, 0)):
                    bref = {}
                    for k in range(KD):
                        def mm(k=k, t=t, which=which, bref=bref, s=s):
                            if k == 0:
                                bref["b"] = nextpb()
                            b = bref["b"]
                            if fox:
                                wap, wtok = wqk[s][:, k, :], ("wqk", s)
                            else:
                                wap, wtok = wqk[s][which][:, k, :], ("wqk", s, which)
                            S.op("pe", lambda e: e.matmul(bank(b), lhsT=wap, rhs=hT[:, k, tsl(t)],
                                                           start=(k == 0), stop=(k == KD - 1)),
                                 reads=[("h", k, t), wtok], writes=[PSK(b)])
                        items.append(mm)
                    if fox:
                        def evq(t=t, bref=bref, s=s):
                            b = bref["b"]
                            S.op("dve", lambda e: e.tensor_copy(out=qaug[s][0:HD, tsl(t)], in_=bank(b, 0, HD)),
                                 reads=[PSK(b)], writes=[("q", s, t)])

                        def evk(t=t, bref=bref, s=s):
                            b = bref["b"]
                            r = nkst[0] % 2
                            nkst[0] += 1
                            bref["r"] = r
                            S.op("dve", lambda e: e.tensor_copy(out=kst[r][64:128, :], in_=bank(b, 64, 128)),
                                 reads=[PSK(b)], writes=[("kst", r)])

                        def evd(t=t, bref=bref, s=s):
                            r = bref["r"]
                            S.op("sp", lambda e: e.dma_start(out=kaug[s][0:HD, tsl(t)], in_=kst[r][64:128, :]),
                                 reads=[("kst", r)], writes=[("k", s, t)], dma=True)
                        items += [evq, evk, evd]
                    elif which == 0:
                        def evq(t=t, bref=bref, s=s):
                            b = bref["b"]
                            S.op("dve", lambda e: e.tensor_copy(out=qaug[s][:, tsl(t)], in_=bank(b)),
                                 reads=[PSK(b)], writes=[("q", s, t)])
                        items.append(evq)
                    else:
                        def evk0(t=t, bref=bref, s=s):
                            b = bref["b"]
                            S.op("act", lambda e: e.activation(out=kpad[s][0][0:64, tsl(t)], in_=bank(b, 0, 64), func=AF.Copy),
                                 reads=[PSK(b)], writes=[("k", s, t, 0)])

                        def evk1(t=t, bref=bref, s=s):
                            b = bref["b"]
                            S.op("act", lambda e: e.activation(out=kpad[s][1][64:128, tsl(t)], in_=bank(b, 64, 128), func=AF.Copy),
                                 reads=[PSK(b)], writes=[("k", s, t, 1)])
                        items += [evk0, evk1]
            return items

        def crow_dmas(h):
            s = h % 2
            for i in range(3):
                S.op("sp", lambda e, s=s, h=h, i=i: e.dma_start(out=qaug[s][64 + i:65 + i, :], in_=csplit[h:h + 1, i, :]),
                     reads=[("cs", i, t) for t in range(NTC)], writes=[("qc", s, i)], dma=True)

        load_unit(0)
        for it_ in make_proj_items(0):
            it_()
        if fox:
            crow_dmas(0)
        for u, heads in enumerate(units):
          s = u % 2
          nxt = []
          if u + 1 < len(units):
              load_unit(u + 1)
              nxt = make_proj_items(u + 1)
          rate = 2 if fox else 1
          for h in heads:
            pc = h // 2
            half = h % 2
            blocks = [(c, j) for c in range(NTC) for j in range(4 * c + 4)]
            LA = 3
            nb = len(blocks)
            info = {}
            for i in range(nb + LA):
                if i < nb:
                    c, j = blocks[i]
                    m = j - 4 * c if j >= 4 * c else 0
                    col0 = 128 * m
                    sb = SB[nblk[0] % 4]
                    pi = nblk[0] % NPT
                    nblk[0] += 1
                    info[i] = (c, j, col0, sb, pi)
                    if fox:
                        rd = [("k", s, j // 4), ("q", s, c), ("kones", s)] + [("qc", s, ii) for ii in range(3)]
                        S.op("pe", lambda e, s=s, c=c, j=j, col0=col0, sb=sb: e.matmul(
                            bank(sb, 0, 128, col0, TC), lhsT=kaug[s][0:KQ, j * 128:(j + 1) * 128],
                            rhs=qaug[s][0:KQ, c * TC + col0:(c + 1) * TC], start=True, stop=True),
                             reads=rd, writes=[PSK(sb)])
                    else:
                        rd = [("k", s, j // 4, half), ("kz", s, half), ("q", s, c)]
                        S.op("pe", lambda e, s=s, c=c, j=j, col0=col0, sb=sb, half=half: e.matmul(
                            bank(sb, 0, 128, col0, TC), lhsT=kpad[s][half][:, j * 128:(j + 1) * 128],
                            rhs=qaug[s][:, c * TC + col0:(c + 1) * TC], start=True, stop=True),
                             reads=rd, writes=[PSK(sb)])
                    if fox:
                        S.op("act", lambda e, sb=sb, pi=pi, col0=col0, j=j, h=h: e.activation(
                            out=pT[pi][:, col0:TC], in_=bank(sb, 0, 128, col0, TC), func=AF.Exp, scale=0.125,
                            bias=dkT[:, j * NH + h:j * NH + h + 1]),
                             reads=[PSK(sb), "dkT"], writes=[("pT", pi)])
                        if j >= 4 * c:
                            S.op("dve", lambda e, pi=pi, col0=col0: e.tensor_tensor(
                                out=pT[pi][:, col0:col0 + 128], in0=pT[pi][:, col0:col0 + 128], in1=tri_b[:], op=ALU.mult),
                                 reads=[("pT", pi), "c_tri"], writes=[("pT", pi)])
                    else:
                        S.op("act", lambda e, sb=sb, pi=pi, col0=col0: e.activation(
                            out=pT[pi][:, col0:TC], in_=bank(sb, 0, 128, col0, TC), func=AF.Exp, scale=0.125),
                             reads=[PSK(sb)], writes=[("pT", pi)])
                        x0 = 512 * c - 128 * j + 384 + col0
                        S.op("dve" if (nblk[0] % 2 == 0) else "pool", lambda e, pi=pi, col0=col0, x0=x0: e.tensor_tensor(
                            out=pT[pi][:, col0:TC], in0=pT[pi][:, col0:TC], in1=ttd_b[:, x0:x0 + TC - col0], op=ALU.mult),
                             reads=[("pT", pi), "c_ttd"], writes=[("pT", pi)])
                if i >= LA:
                    c, j, col0, sb, pi = info[i - LA]
                    ob = OB[c % 2]
                    last = (j == 4 * c + 3)
                    S.op("pe", lambda e, j=j, h=h, col0=col0, pi=pi, ob=ob, last=last: e.matmul(
                        bank(ob, 0, 128, col0, TC), lhsT=vaug[:, j, h, :], rhs=pT[pi][:, col0:TC], start=(j == 0), stop=last),
                         reads=[("pT", pi), ("v", j, half), "vones", "vones2"], writes=[PSK(ob)])
                    if last:
                        ni = nnorm[0] % 2
                        nnorm[0] += 1
                        rn = slice(0, 64) if half == 0 else slice(64, 128)
                        rl = slice(64, 128) if half == 0 else slice(0, 64)
                        S.op("act", lambda e, ni=ni, ob=ob: e.activation(out=numsb[ni][:], in_=bank(ob), func=AF.Copy),
                             reads=[PSK(ob)], writes=[("num", ni)])
                        S.op("sp", lambda e, ni=ni, rn=rn, rl=rl: e.dma_start(out=bc[ni][rn, :], in_=numsb[ni][rl, :]),
                             reads=[("num", ni)], writes=[("bc", ni)], dma=True)

                        for q_ in range(4):
                            def fin(ni=ni, rn=rn, pc=pc, c=c, half=half, q_=q_):
                                cs_ = slice(q_ * 128, (q_ + 1) * 128)
                                S.op("dve", lambda e: e.reciprocal(out=bc[ni][rn, cs_], in_=bc[ni][rn, cs_]),
                                     reads=[("bc", ni)], writes=[("bc", ni)])
                                S.op("dve", lambda e: e.tensor_tensor(
                                    out=yTg[rn, pc, c * TC + q_ * 128:c * TC + (q_ + 1) * 128], in0=numsb[ni][rn, cs_],
                                    in1=bc[ni][rn, cs_], op=ALU.mult),
                                     reads=[("num", ni), ("bc", ni)], writes=[("yg", pc, c, half)])
                            pending.append([nblk[0] + 5 + q_, fin])
                if pending and nblk[0] >= pending[0][0]:
                    pending.pop(0)[1]()
                for _ in range(rate):
                    if nxt:
                        nxt.pop(0)()
          while nxt:
              nxt.pop(0)()
          if fox and u + 1 < len(units):
              crow_dmas(u + 1)
        while pending:
            pending.pop(0)[1]()
        gt = P["gfox"] if fox else P["gdil"]
        S.barrier()
        group_norm_wo(l, yTg, 3, gt, ("par", l, "gfox" if fox else "gdil"), 384, 0 if fox else 3, [6, 7, 0, 1], wo_t,
                      t_outer=(post is not None))
        if post is not None:
            post()
        S.barrier()

    def ffn(l, norm_done=False, post=None):
        A.reset(scope_base)
        P = PL[l]
        G = 6
        wup = [A.alloc("wup", [128, KD, 256], BF16) for _ in range(3)]
        wd = A.alloc("wd", [128, G, D_MODEL], BF16)
        hid = [A.alloc("hid", [128, T], BF16) for _ in range(G)]
        At = [A.alloc("At", [128, T], F32) for _ in range(2)]
        Bt = [A.alloc("Bt", [128, T], F32) for _ in range(2)]
        Gt = [A.alloc("Gt", [128, T], BF16) for _ in range(2)]
        if not norm_done:
            rmsnorm_to_h(P["ln2"], ("par", l, "ln2"), [0, 1, 2, 3])
        groups = []
        i0 = 0
        while i0 < NFF:
            groups.append(list(range(i0, min(NFF, i0 + G))))
            i0 += G
        nw = 0
        for grp in groups:
            for il, i in enumerate(grp):
                ws = nw % 3
                r2 = nw % 2
                nw += 1
                ld("pool", wup[ws][:], WL[l]["wup"][i], [("wup", ws)])
                for half in range(2):
                    for t in range(NTC):
                        b = half * 4 + t
                        proj(lambda k, ws=ws, half=half: wup[ws][:, k, half * 128:(half + 1) * 128], [("wup", ws)], 128, t, b)
                for half in range(2):
                    fi = half * NFF + i
                    dst = At[r2] if half == 0 else Bt[r2]
                    dtok = ("At", r2) if half == 0 else ("Bt", r2)
                    pbk = [PSK(half * 4 + t) for t in range(NTC)]
                    base = half * 2048
                    S.op("act", lambda e, dst=dst, base=base, fi=fi: e.activation(
                        out=dst[:], in_=ps[:, base:base + T], func=AF.Identity, scale=P["ffnw"][:, fi, 2:3],
                        bias=P["ffnb"][:, fi:fi + 1]),
                         reads=pbk + [("par", l, "ffnw"), ("par", l, "ffnb")], writes=[dtok])
                    S.op("dve", lambda e, dst=dst, base=base, fi=fi: e.scalar_tensor_tensor(
                        out=dst[:, 1:T], in0=ps[:, base:base + T - 1], scalar=P["ffnw"][:, fi, 1:2], in1=dst[:, 1:T],
                        op0=ALU.mult, op1=ALU.add),
                         reads=pbk + [dtok, ("par", l, "ffnw")], writes=[dtok])
                    S.op("dve", lambda e, dst=dst, base=base, fi=fi: e.scalar_tensor_tensor(
                        out=dst[:, 2:T], in0=ps[:, base:base + T - 2], scalar=P["ffnw"][:, fi, 0:1], in1=dst[:, 2:T],
                        op0=ALU.mult, op1=ALU.add),
                         reads=pbk + [dtok, ("par", l, "ffnw")], writes=[dtok])
                    if half == 0:
                        S.op("act", lambda e, r2=r2: e.activation(out=Gt[r2][:], in_=At[r2][:], func=AF.Silu),
                             reads=[("At", r2)], writes=[("Gt", r2)])
                S.op("dve", lambda e, r2=r2, il=il: e.tensor_tensor(out=hid[il][:], in0=Gt[r2][:], in1=Bt[r2][:], op=ALU.mult),
                     reads=[("Gt", r2), ("Bt", r2)], writes=[("hid", il)])
            ng = len(grp)
            ld("pool", wd[:, 0:ng, :], WL[l]["wdown"][:, grp[0]:grp[0] + ng, :], ["wd"])
            n = 0
            lastg = (grp is groups[-1]) and (post is not None)
            order = [(e_, t) for t in range(NTC) for e_ in range(KD)] if lastg else [(e_, t) for e_ in range(KD) for t in range(NTC)]
            for e_, t in order:
                b = n % 8
                n += 1
                for il in range(ng):
                    S.op("pe", lambda e, il=il, e_=e_, t=t, b=b: e.matmul(bank(b), lhsT=wd[:, il, e_ * 128:(e_ + 1) * 128],
                                                                       rhs=hid[il][:, tsl(t)], start=(il == 0),
                                                                       stop=(il == ng - 1)),
                         reads=[("hid", il), "wd"], writes=[PSK(b)])
                S.op("dve", lambda e, e_=e_, t=t, b=b: e.tensor_tensor(out=xT[:, e_, tsl(t)], in0=bank(b), in1=xT[:, e_, tsl(t)],
                                                                   op=ALU.add),
                     reads=[PSK(b), ("x", e_, t)], writes=[("x", e_, t)])
        if post is not None:
            post()
        S.barrier()

    def final_store(s):
        A.reset(scope_base)
        outst = [A.alloc("outst", [128, TC], F32) for _ in range(4)]
        outs = []
        n = 0
        for t in range(NTC):
            b = t % 4
            for c in range(KD):
                i = ctr["sq"] % len(sqring)
                ctr["sq"] += 1
                sq = sqring[i]
                S.op("act", lambda e, sq=sq, c=c, t=t: e.activation(out=sq[:], in_=xT[:, c, tsl(t)], func=AF.Square),
                     reads=[("x", c, t)], writes=[("sq", i)])
                S.op("pe", lambda e, sq=sq, c=c, b=b: e.matmul(bank(b), lhsT=ones_b[:], rhs=sq[:], start=(c == 0), stop=(c == KD - 1)),
                     reads=[("sq", i), "c_ones_b"], writes=[PSK(b)])
            r = ctr["rs"] % len(rsring)
            ctr["rs"] += 1
            rs = rsring[r]
            S.op("act", lambda e, rs=rs, b=b: e.activation(out=rs[:], in_=bank(b), func=AF.Ln, scale=1.0 / D_MODEL, bias=EPS),
                 reads=[PSK(b)], writes=[("rs", r)])
            S.op("act", lambda e, rs=rs: e.activation(out=rs[:], in_=rs[:], func=AF.Exp, scale=-0.5),
                 reads=[("rs", r)], writes=[("rs", r)])
            for c in range(KD):
                oi = n % 4
                n += 1
                S.op("dve", lambda e, rs=rs, c=c, t=t, oi=oi: e.scalar_tensor_tensor(
                    out=outst[oi][:], in0=xT[:, c, tsl(t)], scalar=gfin[:, c:c + 1], in1=rs[:], op0=ALU.mult, op1=ALU.mult),
                     reads=[("x", c, t), ("rs", r), "gfin"], writes=[("outst", oi)])
                outs.append(S.op("sp", lambda e, c=c, t=t, oi=oi: e.dma_start(out=yT_d[s, c, :, tsl(t)], in_=outst[oi][:]),
                                 reads=[("outst", oi)], dma=True))
        S.barrier()
        return outs

    def raw_store(s):
        outs = []
        for c in range(KD):
            outs.append(S.op("sp", lambda e, c=c: e.dma_start(out=yT_d[s, c, :, :], in_=xT[:, c, :]),
                             reads=[("x", c, t) for t in range(NTC)], dma=True))
        return outs

    all_outs = []
    for s in range(n_seq):
        for c in range(KD):
            S.op("sp", lambda e, c=c, s=s: e.dma_start(out=xT[:, c, :], in_=xT_d[s, c, :, :]),
                 writes=[("x", c, t) for t in range(NTC)], dma=True)
        h_ready = False
        for l in range(n_layers):
            if do_mixer:
                A.reset(scope_base)
                if not h_ready:
                    rmsnorm_to_h(PL[l]["ln1"], ("par", l, "ln1"), [0, 1, 2, 3])
                h_ready = False
                conv_branch(l)
                attn_group(l, "fox")
                if do_ffn:
                    attn_group(l, "dil", post=lambda l=l: rmsnorm_to_h(PL[l]["ln2"], ("par", l, "ln2"), [2, 3, 4, 5]))
                else:
                    attn_group(l, "dil")
            if do_ffn:
                if do_mixer and l + 1 < n_layers:
                    ffn(l, norm_done=do_mixer,
                        post=lambda l=l: rmsnorm_to_h(PL[l + 1]["ln1"], ("par", l + 1, "ln1"), [0, 1, 2, 3]))
                    h_ready = True
                else:
                    ffn(l, norm_done=do_mixer)
        if do_final:
            all_outs += final_store(s)
        else:
            all_outs += raw_store(s)
    S.op("sp", None, extra_deps=all_outs)
    S.emit()
    return nc


def _chunk_rows(w):
    kk = w.shape[0] // 128
    return np.ascontiguousarray(w.reshape(kk, 128, w.shape[1]).transpose(1, 0, 2))


def _vec_chunks(v):
    kk = v.shape[0] // 128
    return np.ascontiguousarray(v.reshape(kk, 128).T)


def make_consts():
    p = np.arange(128)[:, None]
    f = np.arange(128)[None, :]
    ident = (p == f).astype(np.float32)
    ones = np.ones((128, 128), np.float32)
    tri = (f >= p).astype(np.float32)
    x = np.arange(TTW)[None, :]
    d = x - 384 - p
    m1 = (d >= 0) & (d <= 128)
    m2 = (d >= 0) & (d % 4 == 0) & (d <= 512)
    m3 = (d >= 0) & (d % 16 == 0) & (d <= 2048)
    ttd = (m1.astype(np.float32) + m2.astype(np.float32) + m3.astype(np.float32))
    return {"c_ident": ident, "c_ones": ones, "c_tri": tri, "c_ttd": np.ascontiguousarray(ttd)}


def prep_layer(inp, l, li):
    w_in = np.asarray(inp["w_in"][l], np.float32)
    wr = _chunk_rows(w_in)
    d = {}

    def heads(off):
        return [np.ascontiguousarray(wr[:, :, off + h * HD:off + (h + 1) * HD]) for h in range(NH)]

    qa, ka, qb, kb = heads(OFF_QA), heads(OFF_KA), heads(OFF_QB), heads(OFF_KB)
    d["wqkf%d" % li] = np.stack([np.concatenate([qa[h], ka[h]], 2) for h in range(NH)], 0)
    d["wqkd%d" % li] = np.stack([np.stack([np.concatenate([qb[2 * p_], qb[2 * p_ + 1]], 2),
                                           np.concatenate([kb[2 * p_], kb[2 * p_ + 1]], 2)], 0) for p_ in range(3)], 0)
    d["wv%d" % li] = np.stack([np.ascontiguousarray(wr[:, :, OFF_VA:OFF_VA + 384]),
                               np.ascontiguousarray(wr[:, :, OFF_VB:OFF_VB + 384])], 0)
    d["wfa%d" % li] = np.ascontiguousarray(wr[:, :, OFF_FA:OFF_FA + NH])
    d["wconv%d" % li] = np.ascontiguousarray(np.concatenate([wr[:, :, OFF_GV:OFF_GV + 256], wr[:, :, OFF_GG:OFF_GG + 256]], 2))
    d["wo%d" % li] = _chunk_rows(np.asarray(inp["w_o"][l], np.float32))
    wup = _chunk_rows(np.asarray(inp["w_up"][l], np.float32))
    d["wup%d" % li] = np.stack([np.concatenate([wup[:, :, i * 128:(i + 1) * 128],
                                                wup[:, :, D_FF + i * 128:D_FF + (i + 1) * 128]], 2) for i in range(NFF)], 0)
    d["wdown%d" % li] = _chunk_rows(np.asarray(inp["w_down"][l], np.float32))
    d["ln1_%d" % li] = _vec_chunks(np.asarray(inp["ln1_g"][l], np.float32))
    d["ln2_%d" % li] = _vec_chunks(np.asarray(inp["ln2_g"][l], np.float32))
    d["gfox%d" % li] = _vec_chunks(np.asarray(inp["g_out_fox"][l], np.float32))
    d["gdil%d" % li] = _vec_chunks(np.asarray(inp["g_out_dil"][l], np.float32))
    cw = np.asarray(inp["conv_w"][l], np.float32)
    d["convw%d" % li] = np.ascontiguousarray(cw.T.reshape(2, 128, CONV_K).transpose(1, 0, 2))
    d["convb%d" % li] = _vec_chunks(np.asarray(inp["conv_b"][l], np.float32))
    d["cng%d" % li] = _vec_chunks(np.asarray(inp["cnorm_g"][l], np.float32))
    d["cnb%d" % li] = _vec_chunks(np.asarray(inp["cnorm_b"][l], np.float32))
    fw = np.asarray(inp["ffn_conv_w"][l], np.float32)
    d["ffnw%d" % li] = np.ascontiguousarray(fw.T.reshape(2 * NFF, 128, 3).transpose(1, 0, 2))
    d["ffnb%d" % li] = _vec_chunks(np.asarray(inp["ffn_conv_b"][l], np.float32))
    d["bf%d" % li] = np.ascontiguousarray(np.asarray(inp["b_forget"][l], np.float32).reshape(NH, 1))
    return d


def x_to_dev(x):
    n = x.shape[0]
    return np.ascontiguousarray(x.transpose(0, 2, 1).reshape(n, KD, 128, T))


def x_from_dev(y):
    n = y.shape[0]
    return np.ascontiguousarray(y.reshape(n, D_MODEL, T).transpose(0, 2, 1))


_PROG_CACHE = {}


def get_prog(key):
    if key not in _PROG_CACHE:
        _PROG_CACHE[key] = build_program(*key)
    return _PROG_CACHE[key]


def kernel(**inputs):
    x = np.asarray(inputs["x"], np.float32)
    B = x.shape[0]
    per = B // N_CORES
    consts = make_consts()
    depth = inputs["w_in"].shape[0]
    common = dict(consts)
    common["g_final"] = _vec_chunks(np.asarray(inputs["g_final"], np.float32))
    for l in range(depth):
        common.update(prep_layer(inputs, l, l))
    nc = get_prog((per, depth, True, True, True))
    xd = x_to_dev(x)
    in_maps = []
    for c in range(N_CORES):
        m = dict(common)
        m["xT"] = xd[c * per:(c + 1) * per]
        in_maps.append(m)
    res = run_bass_kernel_spmd(nc, in_maps, core_ids=list(range(N_CORES)))
    y = np.concatenate([np.asarray(r["yT"]) for r in res.results], 0)
    return x_from_dev(y).astype(np.float32)
```

```python
import contextlib
import numpy as np
import concourse.bass as bass
import concourse.mybir as mybir
from concourse.bass_utils import run_bass_kernel_spmd

F32 = mybir.dt.float32
BF16 = mybir.dt.bfloat16
AF = mybir.ActivationFunctionType
ALU = mybir.AluOpType

D_MODEL = 1024
T = 2048
TC = 512
NTC = 4
KD = 8
HD = 64
NH = 6
W_FOX = 384
W_DIL = 384
CONV_CH = 256
CONV_K = 31
D_FF = 2816
NFF = 22
EPS = 1e-6
OFF_QA = 0
OFF_KA = 384
OFF_VA = 768
OFF_FA = 1152
OFF_QB = 1158
OFF_KB = OFF_QB + 384
OFF_VB = OFF_KB + 384
OFF_GV = OFF_VB + 384
OFF_GG = OFF_GV + 256
N_IN = OFF_GG + 256
TTW = 2432
N_CORES = 8

ENGINES = ("pe", "act", "dve", "pool", "sp")


class Op:
    __slots__ = ("eng", "fn", "dma", "deps", "sig", "needed", "gid")

    def __init__(self, eng, fn, dma, gid):
        self.eng = eng
        self.fn = fn
        self.dma = dma
        self.deps = []
        self.sig = None
        self.needed = False
        self.gid = gid


class Sched:
    def __init__(self, nc, n_dma_sems=32):
        self.nc = nc
        self.ops = {e: [] for e in ENGINES}
        self.tok = {}
        self.gid = 0
        self.n_dma_sems = n_dma_sems
        self.all_dma = []

    def op(self, eng, fn, reads=(), writes=(), dma=False, extra_deps=()):
        o = Op(eng, fn, dma, self.gid)
        self.gid += 1
        deps = {}
        for t in reads:
            st = self.tok.get(t)
            if st is not None and st[0] is not None:
                deps[id(st[0])] = st[0]
        for t in writes:
            st = self.tok.get(t)
            if st is not None:
                if st[0] is not None:
                    deps[id(st[0])] = st[0]
                for r in st[1]:
                    deps[id(r)] = r
        for d in extra_deps:
            deps[id(d)] = d
        for t in reads:
            st = self.tok.get(t)
            if st is None:
                st = [None, []]
                self.tok[t] = st
            st[1].append(o)
        for t in writes:
            self.tok[t] = [o, []]
        dl = []
        for d in deps.values():
            if d is o:
                continue
            if eng == "pe" and d.eng == "pe" and not d.dma and not dma:
                continue
            dl.append(d)
        o.deps = dl
        for d in dl:
            d.needed = True
        self.ops[eng].append(o)
        if dma:
            self.all_dma.append(o)
        return o

    def barrier(self):
        lasts = []
        for e in ENGINES:
            for o in reversed(self.ops[e]):
                if not o.dma and o.fn is not None:
                    lasts.append(o)
                    break
        dmas = {}
        for st in self.tok.values():
            if st[0] is not None and st[0].dma:
                dmas[id(st[0])] = st[0]
            for r in st[1]:
                if r.dma:
                    dmas[id(r)] = r
        deps = lasts + list(dmas.values())
        for e in ENGINES:
            if e == "pe":
                self.op(e, None, extra_deps=[d for d in deps])
            else:
                self.op(e, None, extra_deps=deps)

    def emit(self):
        nc = self.nc
        eng_attr = {"pe": "tensor", "act": "scalar", "dve": "vector", "pool": "gpsimd", "sp": "sync"}
        with contextlib.ExitStack() as es:
            esem = {e: es.enter_context(nc.semaphore("s_" + e)) for e in ENGINES}
            dsem = [es.enter_context(nc.semaphore("d%d" % i)) for i in range(self.n_dma_sems)]
            last_on = [None] * self.n_dma_sems
            cnt = [0] * self.n_dma_sems
            for k, o in enumerate(sorted(self.all_dma, key=lambda x: x.gid)):
                s = k % self.n_dma_sems
                cnt[s] += 16
                o.sig = (dsem[s], cnt[s])
                if last_on[s] is not None:
                    o.deps.append(last_on[s])
                last_on[s] = o
            for e in ENGINES:
                c = 0
                for o in self.ops[e]:
                    if o.dma:
                        continue
                    if o.needed:
                        assert o.fn is not None
                        c += 1
                        o.sig = (esem[e], c)
            block = es.enter_context(nc.Block())
            for e in ENGINES:
                ops = self.ops[e]
                if not ops:
                    continue

                def body(eng, ops=ops):
                    waited = {}
                    for o in ops:
                        for d in o.deps:
                            s, v = d.sig
                            if waited.get(id(s), 0) < v:
                                eng.wait_ge(s, v)
                                waited[id(s)] = v
                        if o.fn is None:
                            continue
                        ins = o.fn(eng)
                        if o.dma:
                            ins.then_inc(o.sig[0], 16)
                        elif o.sig is not None:
                            ins.then_inc(o.sig[0], 1)

                getattr(block, eng_attr[e])(body)


class Arena:
    def __init__(self, nc, limit):
        self.nc = nc
        self.off = 16512
        self.limit = limit
        self.n = 0

    def alloc(self, name, shape, dtype):
        esz = 4 if dtype == F32 else 2
        nb = esz
        for s in shape[1:]:
            nb *= s
        nb = (nb + 63) // 64 * 64
        off = self.off
        self.off += nb
        assert self.off <= self.limit, "SBUF arena overflow at %s: %d > %d" % (name, self.off, self.limit)
        self.n += 1
        return self.nc.alloc_sbuf_tensor_at("%s_%d" % (name, self.n), list(shape), dtype, offset=off)

    def mark(self):
        return self.off

    def reset(self, m):
        self.off = m


def build_program(n_seq, n_layers, do_mixer=True, do_ffn=True, do_final=True):
    nc = bass.Bass("TRN2", target_bir_lowering=False)

    def din(name, shape):
        return nc.dram_tensor(name, list(shape), F32, kind="ExternalInput").ap()

    xT_d = din("xT", [n_seq, KD, 128, T])
    yT_d = nc.dram_tensor("yT", [n_seq, KD, 128, T], F32, kind="ExternalOutput").ap()
    c_ident = din("c_ident", [128, 128])
    c_ones = din("c_ones", [128, 128])
    c_tri = din("c_tri", [128, 128])
    c_ttd = din("c_ttd", [128, TTW])
    gfin_d = din("g_final", [128, KD])
    WL = []
    for l in range(n_layers):
        w = {}
        w["wqkf"] = din("wqkf%d" % l, [NH, 128, KD, 128])
        w["wqkd"] = din("wqkd%d" % l, [3, 2, 128, KD, 128])
        w["wv"] = din("wv%d" % l, [2, 128, KD, 384])
        w["wfa"] = din("wfa%d" % l, [128, KD, NH])
        w["wconv"] = din("wconv%d" % l, [128, KD, 512])
        w["wo"] = din("wo%d" % l, [128, KD, D_MODEL])
        w["wup"] = din("wup%d" % l, [NFF, 128, KD, 256])
        w["wdown"] = din("wdown%d" % l, [128, NFF, D_MODEL])
        w["ln1"] = din("ln1_%d" % l, [128, KD])
        w["ln2"] = din("ln2_%d" % l, [128, KD])
        w["gfox"] = din("gfox%d" % l, [128, 3])
        w["gdil"] = din("gdil%d" % l, [128, 3])
        w["convw"] = din("convw%d" % l, [128, 2, CONV_K])
        w["convb"] = din("convb%d" % l, [128, 2])
        w["cng"] = din("cng%d" % l, [128, 2])
        w["cnb"] = din("cnb%d" % l, [128, 2])
        w["ffnw"] = din("ffnw%d" % l, [128, 2 * NFF, 3])
        w["ffnb"] = din("ffnb%d" % l, [128, 2 * NFF])
        w["bf"] = din("bf%d" % l, [NH, 1])
        WL.append(w)

    S = Sched(nc)
    A = Arena(nc, 229344)
    ps = nc.alloc_psum_tensor("ps", [128, 4096], F32)

    def bank(b, p0=0, p1=128, c0=0, c1=512):
        return ps[p0:p1, b * 512 + c0:b * 512 + c1]

    def PSK(b):
        return ("ps", b)

    xT = A.alloc("xT", [128, KD, T], F32)
    hT = A.alloc("hT", [128, KD, T], BF16)
    ident_f = A.alloc("ident_f", [128, 128], F32)
    ones_f = A.alloc("ones_f", [128, 128], F32)
    ident_b = A.alloc("ident_b", [128, 128], BF16)
    ones_b = A.alloc("ones_b", [128, 128], BF16)
    tri_b = A.alloc("tri_b", [128, 128], BF16)
    ttd_b = A.alloc("ttd_b", [128, TTW], BF16)
    ones512 = A.alloc("ones512", [128, 512], F32)
    gfin = A.alloc("gfin", [128, KD], F32)
    PL = []
    for l in range(n_layers):
        p = {}
        p["ln1"] = A.alloc("ln1", [128, KD], F32)
        p["ln2"] = A.alloc("ln2", [128, KD], F32)
        p["gfox"] = A.alloc("gfox", [128, 3], F32)
        p["gdil"] = A.alloc("gdil", [128, 3], F32)
        p["convw"] = A.alloc("convw", [128, 2, CONV_K], F32)
        p["convb"] = A.alloc("convb", [128, 2], F32)
        p["cng"] = A.alloc("cng", [128, 2], F32)
        p["cnb"] = A.alloc("cnb", [128, 2], F32)
        p["ffnw"] = A.alloc("ffnw", [128, 2 * NFF, 3], F32)
        p["ffnb"] = A.alloc("ffnb", [128, 2 * NFF], F32)
        p["bf"] = A.alloc("bf", [NH, 1], F32)
        p["negb"] = A.alloc("negb", [NH, 1], F32)
        PL.append(p)
    sqring = [A.alloc("sq", [128, TC], BF16) for _ in range(3)]
    rsring = [A.alloc("rs", [128, TC], F32) for _ in range(2)]
    ctr = {"sq": 0, "rs": 0}

    def ld(eng, dst_ap, src_ap, tokw):
        return S.op(eng, lambda e: e.dma_start(out=dst_ap, in_=src_ap), writes=tokw, dma=True)

    ld("sp", ident_f[:], c_ident, ["c_ident_f"])
    ld("sp", ones_f[:], c_ones, ["c_ones_f"])
    ld("pool", ident_b[:], c_ident, ["c_ident_b"])
    ld("pool", ones_b[:], c_ones, ["c_ones_b"])
    ld("pool", tri_b[:], c_tri, ["c_tri"])
    ld("pool", ttd_b[:], c_ttd, ["c_ttd"])
    ld("sp", gfin[:], gfin_d, ["gfin"])
    S.op("dve", lambda e: e.memset(ones512[:], 1.0), writes=["ones512"])
    for l in range(n_layers):
        for k in ("ln1", "ln2", "gfox", "gdil", "convw", "convb", "cng", "cnb", "ffnw", "ffnb", "bf"):
            ld("sp", PL[l][k][:], WL[l][k], [("par", l, k)])
        S.op("dve", lambda e, l=l: e.tensor_scalar(out=PL[l]["negb"][:], in0=PL[l]["bf"][:], scalar1=-1.0,
                                                  scalar2=None, op0=ALU.mult),
             reads=[("par", l, "bf")], writes=[("par", l, "negb")])

    scope_base = A.mark()

    def tsl(t):
        return slice(t * TC, (t + 1) * TC)

    def rmsnorm_to_h(g_tile, gtok, banks):
        for t in range(NTC):
            b = banks[t % len(banks)]
            for c in range(KD):
                i = ctr["sq"] % len(sqring)
                ctr["sq"] += 1
                sq = sqring[i]
                S.op("act", lambda e, sq=sq, c=c, t=t: e.activation(out=sq[:], in_=xT[:, c, tsl(t)], func=AF.Square),
                     reads=[("x", c, t)], writes=[("sq", i)])
                S.op("pe", lambda e, sq=sq, c=c, b=b: e.matmul(bank(b), lhsT=ones_b[:], rhs=sq[:], start=(c == 0),
                                                               stop=(c == KD - 1)),
                     reads=[("sq", i), "c_ones_b"], writes=[PSK(b)])
            r = ctr["rs"] % len(rsring)
            ctr["rs"] += 1
            rs = rsring[r]
            S.op("act", lambda e, rs=rs, b=b: e.activation(out=rs[:], in_=bank(b), func=AF.Ln, scale=1.0 / D_MODEL, bias=EPS),
                 reads=[PSK(b)], writes=[("rs", r)])
            S.op("act", lambda e, rs=rs: e.activation(out=rs[:], in_=rs[:], func=AF.Exp, scale=-0.5),
                 reads=[("rs", r)], writes=[("rs", r)])
            for c in range(KD):
                S.op("dve", lambda e, rs=rs, c=c, t=t: e.scalar_tensor_tensor(out=hT[:, c, tsl(t)], in0=xT[:, c, tsl(t)],
                                                                           scalar=g_tile[:, c:c + 1], in1=rs[:],
                                                                           op0=ALU.mult, op1=ALU.mult),
                     reads=[("x", c, t), ("rs", r), gtok], writes=[("h", c, t)])

    def proj(wfn, wtoks, M, t, b, ncols=TC):
        for k in range(KD):
            S.op("pe", lambda e, k=k: e.matmul(bank(b, 0, M), lhsT=wfn(k), rhs=hT[:, k, tsl(t)], start=(k == 0),
                                               stop=(k == KD - 1)),
                 reads=[("h", k, t)] + wtoks, writes=[PSK(b)])

    def group_norm_wo(l, yTg, npc, gtile, gtok, width, wo_chunk0, banks, wo_t):
        if gtile is not None:
            for t in range(NTC):
                b = banks[t % len(banks)]
                for pc in range(npc):
                    i = ctr["sq"] % len(sqring)
                    ctr["sq"] += 1
                    sq = sqring[i]
                    S.op("act", lambda e, sq=sq, pc=pc, t=t: e.activation(out=sq[:], in_=yTg[:, pc, tsl(t)], func=AF.Square),
                         reads=[("yg", pc, t, 0), ("yg", pc, t, 1)], writes=[("sq", i)])
                    S.op("pe", lambda e, sq=sq, pc=pc, b=b: e.matmul(bank(b), lhsT=ones_b[:], rhs=sq[:], start=(pc == 0),
                                                                     stop=(pc == npc - 1)),
                         reads=[("sq", i), "c_ones_b"], writes=[PSK(b)])
                r = ctr["rs"] % len(rsring)
                ctr["rs"] += 1
                rs = rsring[r]
                S.op("act", lambda e, rs=rs, b=b: e.activation(out=rs[:], in_=bank(b), func=AF.Ln, scale=1.0 / width, bias=EPS),
                     reads=[PSK(b)], writes=[("rs", r)])
                S.op("act", lambda e, rs=rs: e.activation(out=rs[:], in_=rs[:], func=AF.Exp, scale=-0.5),
                     reads=[("rs", r)], writes=[("rs", r)])
                for pc in range(npc):
                    S.op("dve", lambda e, rs=rs, pc=pc, t=t: e.scalar_tensor_tensor(
                        out=yTg[:, pc, tsl(t)], in0=yTg[:, pc, tsl(t)], scalar=gtile[:, pc:pc + 1], in1=rs[:],
                        op0=ALU.mult, op1=ALU.mult),
                         reads=[("yg", pc, t, 0), ("yg", pc, t, 1), ("rs", r), gtok],
                         writes=[("yg", pc, t, 0), ("yg", pc, t, 1)])
        ld("pool", wo_t[:, 0:npc, :], WL[l]["wo"][:, wo_chunk0:wo_chunk0 + npc, :], ["wo"])
        n = 0
        for e_ in range(KD):
            for t in range(NTC):
                b = banks[n % len(banks)]
                n += 1
                for pc in range(npc):
                    S.op("pe", lambda e, pc=pc, e_=e_, t=t, b=b: e.matmul(bank(b), lhsT=wo_t[:, pc, e_ * 128:(e_ + 1) * 128],
                                                                       rhs=yTg[:, pc, tsl(t)], start=(pc == 0),
                                                                       stop=(pc == npc - 1)),
                         reads=[("yg", pc, t, 0), ("yg", pc, t, 1), "wo"], writes=[PSK(b)])
                S.op("dve", lambda e, e_=e_, t=t, b=b: e.tensor_tensor(out=xT[:, e_, tsl(t)], in0=bank(b), in1=xT[:, e_, tsl(t)],
                                                                   op=ALU.add),
                     reads=[PSK(b), ("x", e_, t)], writes=[("x", e_, t)])

    def conv_branch(l):
        A.reset(scope_base)
        P = PL[l]
        wconv = A.alloc("wconv", [128, KD, 512], BF16)
        ycpad = A.alloc("ycpad", [128, 2, 30 + T], BF16)
        dg = A.alloc("dg", [128, 2, CONV_K, 128], BF16)
        yTg = A.alloc("yTgc", [128, 2, T], BF16)
        wo_t = A.alloc("wo_c", [128, 2, D_MODEL], BF16)
        sg = [A.alloc("sg", [128, TC], F32) for _ in range(2)]
        u = [[A.alloc("u", [128, TC], F32) for _ in range(2)] for _ in range(2)]
        dd = [[A.alloc("dd", [128, TC], F32) for _ in range(2)] for _ in range(2)]
        sq2 = [[A.alloc("sq2", [128, TC], F32) for _ in range(2)] for _ in range(2)]
        ld("pool", wconv[:], WL[l]["wconv"], ["wconv"])
        S.op("dve", lambda e: e.memset(ycpad[:, :, 0:30], 0.0), writes=[("ycpad", 0), ("ycpad", 1)])
        for cc in range(2):
            for k in range(CONV_K):
                S.op("dve", lambda e, cc=cc, k=k: e.tensor_scalar(out=dg[:, cc, k, :], in0=ident_b[:],
                                                                 scalar1=P["convw"][:, cc, k:k + 1], scalar2=None,
                                                                 op0=ALU.mult),
                     reads=["c_ident_b", ("par", l, "convw")], writes=[("dg", cc, k)])
        pb = [0, 1, 2, 3]
        n = 0
        for cc in range(2):
            for t in range(NTC):
                ba = pb[n % 4]
                bb = pb[(n + 1) % 4]
                n += 2
                proj(lambda k, cc=cc: wconv[:, k, cc * 128:(cc + 1) * 128], ["wconv"], 128, t, ba)
                proj(lambda k, cc=cc: wconv[:, k, 256 + cc * 128:256 + (cc + 1) * 128], ["wconv"], 128, t, bb)
                si = (n // 2) % 2
                S.op("act", lambda e, si=si, bb=bb: e.activation(out=sg[si][:], in_=bank(bb), func=AF.Sigmoid),
                     reads=[PSK(bb)], writes=[("sg", si)])
                S.op("dve", lambda e, si=si, ba=ba, cc=cc, t=t: e.tensor_tensor(
                    out=ycpad[:, cc, 30 + t * TC:30 + (t + 1) * TC], in0=bank(ba), in1=sg[si][:], op=ALU.mult),
                     reads=[PSK(ba), ("sg", si)], writes=[("yc", cc, t)])
        for t in range(NTC):
            r2 = t % 2
            for cc in range(2):
                b = pb[(2 * t + cc) % 4]
                for k in range(CONV_K):
                    rd = [("dg", cc, k), ("yc", cc, t), ("ycpad", cc)]
                    if t > 0:
                        rd.append(("yc", cc, t - 1))
                    S.op("pe", lambda e, cc=cc, k=k, t=t, b=b: e.matmul(bank(b), lhsT=dg[:, cc, k, :],
                                                                     rhs=ycpad[:, cc, t * TC + k:t * TC + k + TC],
                                                                     start=(k == 0), stop=(k == CONV_K - 1)),
                         reads=rd, writes=[PSK(b)])
                S.op("act", lambda e, cc=cc, b=b, r2=r2: e.activation(out=u[cc][r2][:], in_=bank(b), func=AF.Identity,
                                                                   bias=P["convb"][:, cc:cc + 1], scale=1.0),
                     reads=[PSK(b), ("par", l, "convb")], writes=[("u", cc, r2)])
            bs = 4 + (t % 2) * 2
            for cc in range(2):
                S.op("pe", lambda e, cc=cc, r2=r2, bs=bs: e.matmul(bank(bs), lhsT=ones_f[:], rhs=u[cc][r2][:], start=(cc == 0),
                                                                stop=(cc == 1)),
                     reads=[("u", cc, r2), "c_ones_f"], writes=[PSK(bs)])
            for cc in range(2):
                S.op("dve", lambda e, cc=cc, r2=r2, bs=bs: e.scalar_tensor_tensor(
                    out=dd[cc][r2][:], in0=bank(bs), scalar=-1.0 / CONV_CH, in1=u[cc][r2][:], op0=ALU.mult, op1=ALU.add),
                     reads=[PSK(bs), ("u", cc, r2)], writes=[("dd", cc, r2)])
                S.op("act", lambda e, cc=cc, r2=r2: e.activation(out=sq2[cc][r2][:], in_=dd[cc][r2][:], func=AF.Square),
                     reads=[("dd", cc, r2)], writes=[("sq2", cc, r2)])
            for cc in range(2):
                S.op("pe", lambda e, cc=cc, r2=r2, bs=bs: e.matmul(bank(bs + 1), lhsT=ones_f[:], rhs=sq2[cc][r2][:],
                                                                start=(cc == 0), stop=(cc == 1)),
                     reads=[("sq2", cc, r2), "c_ones_f"], writes=[PSK(bs + 1)])
            r = ctr["rs"] % len(rsring)
            ctr["rs"] += 1
            rs = rsring[r]
            S.op("act", lambda e, rs=rs, bs=bs: e.activation(out=rs[:], in_=bank(bs + 1), func=AF.Ln, scale=1.0 / CONV_CH, bias=EPS),
                 reads=[PSK(bs + 1)], writes=[("rs", r)])
            S.op("act", lambda e, rs=rs: e.activation(out=rs[:], in_=rs[:], func=AF.Exp, scale=-0.5),
                 reads=[("rs", r)], writes=[("rs", r)])
            for cc in range(2):
                S.op("dve", lambda e, cc=cc, r2=r2, rs=rs: e.tensor_tensor(out=dd[cc][r2][:], in0=dd[cc][r2][:], in1=rs[:],
                                                                        op=ALU.mult),
                     reads=[("dd", cc, r2), ("rs", r)], writes=[("dd", cc, r2)])
                S.op("act", lambda e, cc=cc, r2=r2, t=t: e.activation(out=yTg[:, cc, tsl(t)], in_=dd[cc][r2][:], func=AF.Silu,
                                                                   scale=P["cng"][:, cc:cc + 1], bias=P["cnb"][:, cc:cc + 1]),
                     reads=[("dd", cc, r2), ("par", l, "cng"), ("par", l, "cnb")],
                     writes=[("yg", cc, t, 0), ("yg", cc, t, 1)])
        group_norm_wo(l, yTg, 2, None, None, None, 6, [0, 1, 2, 3], wo_t)
        S.barrier()

    def attn_group(l, kind):
        A.reset(scope_base)
        P = PL[l]
        fox = kind == "fox"
        KQ = 67
        yTg = A.alloc("yTg", [128, 3, T], BF16)
        m_wo = A.mark()
        wo_t = A.alloc("wo_a", [128, 3, D_MODEL], BF16)
        A.reset(m_wo)
        if fox:
            qaug = [A.alloc("qaug", [128, T], BF16) for _ in range(2)]
            kaug = [A.alloc("kaug", [128, T], BF16) for _ in range(2)]
            wqk = [A.alloc("wqk", [128, KD, 128], BF16) for _ in range(2)]
            kst = [A.alloc("kst", [128, TC], BF16) for _ in range(2)]
        else:
            qaug = [A.alloc("qpair", [128, T], BF16) for _ in range(2)]
            kpad = [[A.alloc("kpad", [128, T], BF16) for _ in range(2)] for _ in range(2)]
            wqk = [[A.alloc("wqk", [128, KD, 128], BF16) for _ in range(2)] for _ in range(2)]
        wv = A.alloc("wv", [128, KD, 384], BF16)
        NPT = 7
        pT = [A.alloc("pT", [128, TC], BF16) for _ in range(NPT)]
        numsb = [A.alloc("numsb", [128, TC], F32) for _ in range(2)]
        bc = [A.alloc("bc", [128, TC], F32) for _ in range(2)]
        if fox:
            dkT = A.alloc("dkT", [128, 16 * NH], F32)
            csplit = A.alloc("csplit", [NH, 3, T], BF16)
            m2 = A.mark()
            wfa = A.alloc("wfa", [128, KD, NH], BF16)
            dT = A.alloc("dT", [NH, T], F32)
            e6 = [A.alloc("e6", [NH, TC], F32) for _ in range(2)]
            tA = A.alloc("tA", [NH, TC], F32)
            tB = A.alloc("tB", [NH, TC], F32)
            A.reset(m2)
        vaug = A.alloc("vaug", [128, 16, NH, 128], BF16)
        PB = [5, 6, 7]
        pbn = [0]

        def nextpb():
            b = PB[pbn[0] % len(PB)]
            pbn[0] += 1
            return b

        grp = 0 if fox else 1
        ld("pool", wv[:], WL[l]["wv"][grp], ["wv"])
        for s_ in range(2):
            if fox:
                S.op("dve", lambda e, s_=s_: e.memset(kaug[s_][64:67, :], 1.0), writes=[("kones", s_)])
            else:
                S.op("dve", lambda e, s_=s_: e.memset(kpad[s_][0][64:128, :], 0.0), writes=[("kz", s_, 0)])
                S.op("dve", lambda e, s_=s_: e.memset(kpad[s_][1][0:64, :], 0.0), writes=[("kz", s_, 1)])
        if fox:
            ld("pool", wfa[:], WL[l]["wfa"], ["wfa"])
            for t in range(NTC):
                b = nextpb()
                proj(lambda k: wfa[:, k, :], ["wfa"], NH, t, b)
                ei = t % 2
                S.op("act", lambda e, b=b, ei=ei: e.activation(out=e6[ei][:], in_=bank(b, 0, NH), func=AF.Exp, scale=-1.0,
                                                            bias=P["negb"][:, 0:1]),
                     reads=[PSK(b), ("par", l, "negb")], writes=[("e6", ei)])
                S.op("act", lambda e, ei=ei: e.activation(out=e6[ei][:], in_=e6[ei][:], func=AF.Ln, scale=1.0, bias=1.0),
                     reads=[("e6", ei)], writes=[("e6", ei)])
                init = 0.0 if t == 0 else dT[:, t * TC - 1:t * TC]
                rd = [("e6", ei), "ones512"] + ([("dT", t - 1)] if t > 0 else [])
                S.op("dve", lambda e, ei=ei, t=t, init=init: e.tensor_tensor_scan(
                    out=dT[:, tsl(t)], data0=ones512[0:NH, :], data1=e6[ei][:], initial=init, op0=ALU.mult, op1=ALU.add),
                     reads=rd, writes=[("dT", t)])
                S.op("dve", lambda e, t=t: e.tensor_scalar(out=tA[:], in0=dT[:, tsl(t)], scalar1=-8.0, scalar2=None, op0=ALU.mult),
                     reads=[("dT", t)], writes=["tA"])
                S.op("dve", lambda e, t=t: e.tensor_copy(out=csplit[:, 0, tsl(t)], in_=tA[:]), reads=["tA"], writes=[("cs", 0, t)])
                S.op("dve", lambda e, t=t: e.tensor_tensor(out=tB[:], in0=tA[:], in1=csplit[:, 0, tsl(t)], op=ALU.subtract),
                     reads=["tA", ("cs", 0, t)], writes=["tB"])
                S.op("dve", lambda e, t=t: e.tensor_copy(out=csplit[:, 1, tsl(t)], in_=tB[:]), reads=["tB"], writes=[("cs", 1, t)])
                S.op("dve", lambda e, t=t: e.tensor_tensor(out=tA[:], in0=tB[:], in1=csplit[:, 1, tsl(t)], op=ALU.subtract),
                     reads=["tB", ("cs", 1, t)], writes=["tA"])
                S.op("dve", lambda e, t=t: e.tensor_copy(out=csplit[:, 2, tsl(t)], in_=tA[:]), reads=["tA"], writes=[("cs", 2, t)])
            bt = nextpb()
            for j in range(16):
                S.op("pe", lambda e, j=j, bt=bt: e.transpose(out=bank(bt, 0, 128, j * NH, (j + 1) * NH),
                                                            in_=dT[:, j * 128:(j + 1) * 128], identity=ident_f[0:NH, 0:NH]),
                     reads=[("dT", j // 4), "c_ident_f"], writes=[PSK(bt)])
            S.op("dve", lambda e, bt=bt: e.tensor_copy(out=dkT[:], in_=bank(bt, 0, 128, 0, 16 * NH)),
                 reads=[PSK(bt)], writes=["dkT"])
            S.barrier()
        S.op("dve", lambda e: e.memset(vaug[:, :, 0:NH:2, 64:128], 1.0), writes=["vones"])
        S.op("dve", lambda e: e.memset(vaug[:, :, 1:NH:2, 0:64], 1.0), writes=["vones2"])
        for j in range(16):
            b = nextpb()
            for k in range(KD):
                S.op("pe", lambda e, j=j, k=k, b=b: e.matmul(bank(b, 0, 128, 0, 384), lhsT=hT[:, k, j * 128:(j + 1) * 128],
                                                           rhs=wv[:, k, :], start=(k == 0), stop=(k == KD - 1)),
                     reads=[("h", k, j // 4), "wv"], writes=[PSK(b)])
            S.op("act", lambda e, j=j, b=b: e.activation(
                out=vaug[:, j, 0:NH:2, 0:64],
                in_=bank(b, 0, 128, 0, 384).rearrange("p (e two d) -> p e two d", two=2, d=HD)[:, :, 0, :], func=AF.Copy),
                 reads=[PSK(b)], writes=[("v", j, 0)])
            S.op("act", lambda e, j=j, b=b: e.activation(
                out=vaug[:, j, 1:NH:2, 64:128],
                in_=bank(b, 0, 128, 0, 384).rearrange("p (e two d) -> p e two d", two=2, d=HD)[:, :, 1, :], func=AF.Copy),
                 reads=[PSK(b)], writes=[("v", j, 1)])

        wsrc = WL[l]["wqkf"] if fox else WL[l]["wqkd"]
        SB = [0, 1, 2, 3, 4]
        OB = [5, 6]
        nblk = [0]
        nnorm = [0]
        nkst = [0]
        pending = []
        units = [[h] for h in range(NH)] if fox else [[0, 1], [2, 3], [4, 5]]

        def load_unit(u):
            s = u % 2
            if fox:
                ld("pool", wqk[s][:], wsrc[u], [("wqk", s)])
            else:
                ld("pool", wqk[s][0][:], wsrc[u, 0], [("wqk", s, 0)])
                ld("pool", wqk[s][1][:], wsrc[u, 1], [("wqk", s, 1)])

        def make_proj_items(u):
            s = u % 2
            items = []
            for t in range(NTC):
                for which in ((0,) if fox else (1, 0)):
                    bref = {}
                    for k in range(KD):
                        def mm(k=k, t=t, which=which, bref=bref, s=s):
                            if k == 0:
                                bref["b"] = nextpb()
                            b = bref["b"]
                            if fox:
                                wap, wtok = wqk[s][:, k, :], ("wqk", s)
                            else:
                                wap, wtok = wqk[s][which][:, k, :], ("wqk", s, which)
                            S.op("pe", lambda e: e.matmul(bank(b), lhsT=wap, rhs=hT[:, k, tsl(t)],
                                                           start=(k == 0), stop=(k == KD - 1)),
                                 reads=[("h", k, t), wtok], writes=[PSK(b)])
                        items.append(mm)
                    if fox:
                        def evq(t=t, bref=bref, s=s):
                            b = bref["b"]
                            S.op("dve", lambda e: e.tensor_copy(out=qaug[s][0:HD, tsl(t)], in_=bank(b, 0, HD)),
                                 reads=[PSK(b)], writes=[("q", s, t)])

                        def evk(t=t, bref=bref, s=s):
                            b = bref["b"]
                            r = nkst[0] % 2
                            nkst[0] += 1
                            bref["r"] = r
                            S.op("dve", lambda e: e.tensor_copy(out=kst[r][64:128, :], in_=bank(b, 64, 128)),
                                 reads=[PSK(b)], writes=[("kst", r)])

                        def evd(t=t, bref=bref, s=s):
                            r = bref["r"]
                            S.op("sp", lambda e: e.dma_start(out=kaug[s][0:HD, tsl(t)], in_=kst[r][64:128, :]),
                                 reads=[("kst", r)], writes=[("k", s, t)], dma=True)
                        items += [evq, evk, evd]
                    elif which == 0:
                        def evq(t=t, bref=bref, s=s):
                            b = bref["b"]
                            S.op("dve", lambda e: e.tensor_copy(out=qaug[s][:, tsl(t)], in_=bank(b)),
                                 reads=[PSK(b)], writes=[("q", s, t)])
                        items.append(evq)
                    else:
                        def evk0(t=t, bref=bref, s=s):
                            b = bref["b"]
                            S.op("act", lambda e: e.activation(out=kpad[s][0][0:64, tsl(t)], in_=bank(b, 0, 64), func=AF.Copy),
                                 reads=[PSK(b)], writes=[("k", s, t, 0)])

                        def evk1(t=t, bref=bref, s=s):
                            b = bref["b"]
                            S.op("act", lambda e: e.activation(out=kpad[s][1][64:128, tsl(t)], in_=bank(b, 64, 128), func=AF.Copy),
                                 reads=[PSK(b)], writes=[("k", s, t, 1)])
                        items += [evk0, evk1]
            return items

        def crow_dmas(h):
            s = h % 2
            for i in range(3):
                S.op("sp", lambda e, s=s, h=h, i=i: e.dma_start(out=qaug[s][64 + i:65 + i, :], in_=csplit[h:h + 1, i, :]),
                     reads=[("cs", i, t) for t in range(NTC)], writes=[("qc", s, i)], dma=True)

        load_unit(0)
        for it_ in make_proj_items(0):
            it_()
        if fox:
            crow_dmas(0)
        PB[:] = [7]
        for u, heads in enumerate(units):
          s = u % 2
          nxt = []
          if u + 1 < len(units):
              load_unit(u + 1)
              nxt = make_proj_items(u + 1)
          rate = 2 if fox else 1
          for h in heads:
            pc = h // 2
            half = h % 2
            blocks = [(c, j) for c in range(NTC) for j in range(4 * c + 4)]
            LA = 4
            nb = len(blocks)
            info = {}
            for i in range(nb + LA):
                if i < nb:
                    c, j = blocks[i]
                    m = j - 4 * c if j >= 4 * c else 0
                    col0 = 128 * m
                    sb = SB[nblk[0] % 5]
                    pi = nblk[0] % NPT
                    nblk[0] += 1
                    info[i] = (c, j, col0, sb, pi)
                    if fox:
                        rd = [("k", s, j // 4), ("q", s, c), ("kones", s)] + [("qc", s, ii) for ii in range(3)]
                        S.op("pe", lambda e, s=s, c=c, j=j, col0=col0, sb=sb: e.matmul(
                            bank(sb, 0, 128, col0, TC), lhsT=kaug[s][0:KQ, j * 128:(j + 1) * 128],
                            rhs=qaug[s][0:KQ, c * TC + col0:(c + 1) * TC], start=True, stop=True),
                             reads=rd, writes=[PSK(sb)])
                    else:
                        rd = [("k", s, j // 4, half), ("kz", s, half), ("q", s, c)]
                        S.op("pe", lambda e, s=s, c=c, j=j, col0=col0, sb=sb, half=half: e.matmul(
                            bank(sb, 0, 128, col0, TC), lhsT=kpad[s][half][:, j * 128:(j + 1) * 128],
                            rhs=qaug[s][:, c * TC + col0:(c + 1) * TC], start=True, stop=True),
                             reads=rd, writes=[PSK(sb)])
                    if fox:
                        S.op("act", lambda e, sb=sb, pi=pi, col0=col0, j=j, h=h: e.activation(
                            out=pT[pi][:, col0:TC], in_=bank(sb, 0, 128, col0, TC), func=AF.Exp, scale=0.125,
                            bias=dkT[:, j * NH + h:j * NH + h + 1]),
                             reads=[PSK(sb), "dkT"], writes=[("pT", pi)])
                        if j >= 4 * c:
                            S.op("dve", lambda e, pi=pi, col0=col0: e.tensor_tensor(
                                out=pT[pi][:, col0:col0 + 128], in0=pT[pi][:, col0:col0 + 128], in1=tri_b[:], op=ALU.mult),
                                 reads=[("pT", pi), "c_tri"], writes=[("pT", pi)])
                    else:
                        S.op("act", lambda e, sb=sb, pi=pi, col0=col0: e.activation(
                            out=pT[pi][:, col0:TC], in_=bank(sb, 0, 128, col0, TC), func=AF.Exp, scale=0.125),
                             reads=[PSK(sb)], writes=[("pT", pi)])
                        x0 = 512 * c - 128 * j + 384 + col0
                        S.op("dve" if (nblk[0] % 2 == 0) else "pool", lambda e, pi=pi, col0=col0, x0=x0: e.tensor_tensor(
                            out=pT[pi][:, col0:TC], in0=pT[pi][:, col0:TC], in1=ttd_b[:, x0:x0 + TC - col0], op=ALU.mult),
                             reads=[("pT", pi), "c_ttd"], writes=[("pT", pi)])
                if i >= LA:
                    c, j, col0, sb, pi = info[i - LA]
                    ob = OB[c % 2]
                    last = (j == 4 * c + 3)
                    S.op("pe", lambda e, j=j, h=h, col0=col0, pi=pi, ob=ob, last=last: e.matmul(
                        bank(ob, 0, 128, col0, TC), lhsT=vaug[:, j, h, :], rhs=pT[pi][:, col0:TC], start=(j == 0), stop=last),
                         reads=[("pT", pi), ("v", j, half), "vones", "vones2"], writes=[PSK(ob)])
                    if last:
                        ni = nnorm[0] % 2
                        nnorm[0] += 1
                        rn = slice(0, 64) if half == 0 else slice(64, 128)
                        rl = slice(64, 128) if half == 0 else slice(0, 64)
                        S.op("act", lambda e, ni=ni, ob=ob: e.activation(out=numsb[ni][:], in_=bank(ob), func=AF.Copy),
                             reads=[PSK(ob)], writes=[("num", ni)])
                        S.op("sp", lambda e, ni=ni, rn=rn, rl=rl: e.dma_start(out=bc[ni][rn, :], in_=numsb[ni][rl, :]),
                             reads=[("num", ni)], writes=[("bc", ni)], dma=True)

                        def fin(ni=ni, rn=rn, pc=pc, c=c, half=half):
                            S.op("dve", lambda e: e.reciprocal(out=bc[ni][rn, :], in_=bc[ni][rn, :]),
                                 reads=[("bc", ni)], writes=[("bc", ni)])
                            S.op("dve", lambda e: e.tensor_tensor(
                                out=yTg[rn, pc, tsl(c)], in0=numsb[ni][rn, :], in1=bc[ni][rn, :], op=ALU.mult),
                                 reads=[("num", ni), ("bc", ni)], writes=[("yg", pc, c, half)])
                        pending.append([nblk[0] + 6, fin])
                if pending and nblk[0] >= pending[0][0]:
                    pending.pop(0)[1]()
                for _ in range(rate):
                    if nxt:
                        nxt.pop(0)()
          while nxt:
              nxt.pop(0)()
          if fox and u + 1 < len(units):
              crow_dmas(u + 1)
        while pending:
            pending.pop(0)[1]()
        gt = P["gfox"] if fox else P["gdil"]
        S.barrier()
        group_norm_wo(l, yTg, 3, gt, ("par", l, "gfox" if fox else "gdil"), 384, 0 if fox else 3, [6, 7, 0, 1], wo_t)
        S.barrier()

    def ffn(l):
        A.reset(scope_base)
        P = PL[l]
        G = 6
        wup = [A.alloc("wup", [128, KD, 256], BF16) for _ in range(3)]
        wd = A.alloc("wd", [128, G, D_MODEL], BF16)
        hid = [A.alloc("hid", [128, T], BF16) for _ in range(G)]
        At = [A.alloc("At", [128, T], F32) for _ in range(2)]
        Bt = [A.alloc("Bt", [128, T], F32) for _ in range(2)]
        Gt = [A.alloc("Gt", [128, T], BF16) for _ in range(2)]
        rmsnorm_to_h(P["ln2"], ("par", l, "ln2"), [0, 1, 2, 3])
        groups = []
        i0 = 0
        while i0 < NFF:
            groups.append(list(range(i0, min(NFF, i0 + G))))
            i0 += G
        nw = 0
        for grp in groups:
            for il, i in enumerate(grp):
                ws = nw % 3
                r2 = nw % 2
                nw += 1
                ld("pool", wup[ws][:], WL[l]["wup"][i], [("wup", ws)])
                for half in range(2):
                    for t in range(NTC):
                        b = half * 4 + t
                        proj(lambda k, ws=ws, half=half: wup[ws][:, k, half * 128:(half + 1) * 128], [("wup", ws)], 128, t, b)
                for half in range(2):
                    fi = half * NFF + i
                    dst = At[r2] if half == 0 else Bt[r2]
                    dtok = ("At", r2) if half == 0 else ("Bt", r2)
                    pbk = [PSK(half * 4 + t) for t in range(NTC)]
                    base = half * 2048
                    S.op("act", lambda e, dst=dst, base=base, fi=fi: e.activation(
                        out=dst[:], in_=ps[:, base:base + T], func=AF.Identity, scale=P["ffnw"][:, fi, 2:3],
                        bias=P["ffnb"][:, fi:fi + 1]),
                         reads=pbk + [("par", l, "ffnw"), ("par", l, "ffnb")], writes=[dtok])
                    S.op("dve", lambda e, dst=dst, base=base, fi=fi: e.scalar_tensor_tensor(
                        out=dst[:, 1:T], in0=ps[:, base:base + T - 1], scalar=P["ffnw"][:, fi, 1:2], in1=dst[:, 1:T],
                        op0=ALU.mult, op1=ALU.add),
                         reads=pbk + [dtok, ("par", l, "ffnw")], writes=[dtok])
                    S.op("dve", lambda e, dst=dst, base=base, fi=fi: e.scalar_tensor_tensor(
                        out=dst[:, 2:T], in0=ps[:, base:base + T - 2], scalar=P["ffnw"][:, fi, 0:1], in1=dst[:, 2:T],
                        op0=ALU.mult, op1=ALU.add),
                         reads=pbk + [dtok, ("par", l, "ffnw")], writes=[dtok])
                    if half == 0:
                        S.op("act", lambda e, r2=r2: e.activation(out=Gt[r2][:], in_=At[r2][:], func=AF.Silu),
                             reads=[("At", r2)], writes=[("Gt", r2)])
                S.op("dve", lambda e, r2=r2, il=il: e.tensor_tensor(out=hid[il][:], in0=Gt[r2][:], in1=Bt[r2][:], op=ALU.mult),
                     reads=[("Gt", r2), ("Bt", r2)], writes=[("hid", il)])
            ng = len(grp)
            ld("pool", wd[:, 0:ng, :], WL[l]["wdown"][:, grp[0]:grp[0] + ng, :], ["wd"])
            n = 0
            for e_ in range(KD):
                for t in range(NTC):
                    b = n % 8
                    n += 1
                    for il in range(ng):
                        S.op("pe", lambda e, il=il, e_=e_, t=t, b=b: e.matmul(bank(b), lhsT=wd[:, il, e_ * 128:(e_ + 1) * 128],
                                                                           rhs=hid[il][:, tsl(t)], start=(il == 0),
                                                                           stop=(il == ng - 1)),
                             reads=[("hid", il), "wd"], writes=[PSK(b)])
                    S.op("dve", lambda e, e_=e_, t=t, b=b: e.tensor_tensor(out=xT[:, e_, tsl(t)], in0=bank(b), in1=xT[:, e_, tsl(t)],
                                                                       op=ALU.add),
                         reads=[PSK(b), ("x", e_, t)], writes=[("x", e_, t)])
        S.barrier()

    def final_store(s):
        A.reset(scope_base)
        outst = [A.alloc("outst", [128, TC], F32) for _ in range(4)]
        outs = []
        n = 0
        for t in range(NTC):
            b = t % 4
            for c in range(KD):
                i = ctr["sq"] % len(sqring)
                ctr["sq"] += 1
                sq = sqring[i]
                S.op("act", lambda e, sq=sq, c=c, t=t: e.activation(out=sq[:], in_=xT[:, c, tsl(t)], func=AF.Square),
                     reads=[("x", c, t)], writes=[("sq", i)])
                S.op("pe", lambda e, sq=sq, c=c, b=b: e.matmul(bank(b), lhsT=ones_b[:], rhs=sq[:], start=(c == 0), stop=(c == KD - 1)),
                     reads=[("sq", i), "c_ones_b"], writes=[PSK(b)])
            r = ctr["rs"] % len(rsring)
            ctr["rs"] += 1
            rs = rsring[r]
            S.op("act", lambda e, rs=rs, b=b: e.activation(out=rs[:], in_=bank(b), func=AF.Ln, scale=1.0 / D_MODEL, bias=EPS),
                 reads=[PSK(b)], writes=[("rs", r)])
            S.op("act", lambda e, rs=rs: e.activation(out=rs[:], in_=rs[:], func=AF.Exp, scale=-0.5),
                 reads=[("rs", r)], writes=[("rs", r)])
            for c in range(KD):
                oi = n % 4
                n += 1
                S.op("dve", lambda e, rs=rs, c=c, t=t, oi=oi: e.scalar_tensor_tensor(
                    out=outst[oi][:], in0=xT[:, c, tsl(t)], scalar=gfin[:, c:c + 1], in1=rs[:], op0=ALU.mult, op1=ALU.mult),
                     reads=[("x", c, t), ("rs", r), "gfin"], writes=[("outst", oi)])
                outs.append(S.op("sp", lambda e, c=c, t=t, oi=oi: e.dma_start(out=yT_d[s, c, :, tsl(t)], in_=outst[oi][:]),
                                 reads=[("outst", oi)], dma=True))
        S.barrier()
        return outs

    def raw_store(s):
        outs = []
        for c in range(KD):
            outs.append(S.op("sp", lambda e, c=c: e.dma_start(out=yT_d[s, c, :, :], in_=xT[:, c, :]),
                             reads=[("x", c, t) for t in range(NTC)], dma=True))
        return outs

    all_outs = []
    for s in range(n_seq):
        for c in range(KD):
            S.op("sp", lambda e, c=c, s=s: e.dma_start(out=xT[:, c, :], in_=xT_d[s, c, :, :]),
                 writes=[("x", c, t) for t in range(NTC)], dma=True)
        for l in range(n_layers):
            if do_mixer:
                A.reset(scope_base)
                rmsnorm_to_h(PL[l]["ln1"], ("par", l, "ln1"), [0, 1, 2, 3])
                conv_branch(l)
                attn_group(l, "fox")
                attn_group(l, "dil")
            if do_ffn:
                ffn(l)
        if do_final:
            all_outs += final_store(s)
        else:
            all_outs += raw_store(s)
    S.op("sp", None, extra_deps=all_outs)
    S.emit()
    return nc


def _chunk_rows(w):
    kk = w.shape[0] // 128
    return np.ascontiguousarray(w.reshape(kk, 128, w.shape[1]).transpose(1, 0, 2))


def _vec_chunks(v):
    kk = v.shape[0] // 128
    return np.ascontiguousarray(v.reshape(kk, 128).T)


def make_consts():
    p = np.arange(128)[:, None]
    f = np.arange(128)[None, :]
    ident = (p == f).astype(np.float32)
    ones = np.ones((128, 128), np.float32)
    tri = (f >= p).astype(np.float32)
    x = np.arange(TTW)[None, :]
    d = x - 384 - p
    m1 = (d >= 0) & (d <= 128)
    m2 = (d >= 0) & (d % 4 == 0) & (d <= 512)
    m3 = (d >= 0) & (d % 16 == 0) & (d <= 2048)
    ttd = (m1.astype(np.float32) + m2.astype(np.float32) + m3.astype(np.float32))
    return {"c_ident": ident, "c_ones": ones, "c_tri": tri, "c_ttd": np.ascontiguousarray(ttd)}


def prep_layer(inp, l, li):
    w_in = np.asarray(inp["w_in"][l], np.float32)
    wr = _chunk_rows(w_in)
    d = {}

    def heads(off):
        return [np.ascontiguousarray(wr[:, :, off + h * HD:off + (h + 1) * HD]) for h in range(NH)]

    qa, ka, qb, kb = heads(OFF_QA), heads(OFF_KA), heads(OFF_QB), heads(OFF_KB)
    d["wqkf%d" % li] = np.stack([np.concatenate([qa[h], ka[h]], 2) for h in range(NH)], 0)
    d["wqkd%d" % li] = np.stack([np.stack([np.concatenate([qb[2 * p_], qb[2 * p_ + 1]], 2),
                                           np.concatenate([kb[2 * p_], kb[2 * p_ + 1]], 2)], 0) for p_ in range(3)], 0)
    d["wv%d" % li] = np.stack([np.ascontiguousarray(wr[:, :, OFF_VA:OFF_VA + 384]),
                               np.ascontiguousarray(wr[:, :, OFF_VB:OFF_VB + 384])], 0)
    d["wfa%d" % li] = np.ascontiguousarray(wr[:, :, OFF_FA:OFF_FA + NH])
    d["wconv%d" % li] = np.ascontiguousarray(np.concatenate([wr[:, :, OFF_GV:OFF_GV + 256], wr[:, :, OFF_GG:OFF_GG + 256]], 2))
    d["wo%d" % li] = _chunk_rows(np.asarray(inp["w_o"][l], np.float32))
    wup = _chunk_rows(np.asarray(inp["w_up"][l], np.float32))
    d["wup%d" % li] = np.stack([np.concatenate([wup[:, :, i * 128:(i + 1) * 128],
                                                wup[:, :, D_FF + i * 128:D_FF + (i + 1) * 128]], 2) for i in range(NFF)], 0)
    d["wdown%d" % li] = _chunk_rows(np.asarray(inp["w_down"][l], np.float32))
    d["ln1_%d" % li] = _vec_chunks(np.asarray(inp["ln1_g"][l], np.float32))
    d["ln2_%d" % li] = _vec_chunks(np.asarray(inp["ln2_g"][l], np.float32))
    d["gfox%d" % li] = _vec_chunks(np.asarray(inp["g_out_fox"][l], np.float32))
    d["gdil%d" % li] = _vec_chunks(np.asarray(inp["g_out_dil"][l], np.float32))
    cw = np.asarray(inp["conv_w"][l], np.float32)
    d["convw%d" % li] = np.ascontiguousarray(cw.T.reshape(2, 128, CONV_K).transpose(1, 0, 2))
    d["convb%d" % li] = _vec_chunks(np.asarray(inp["conv_b"][l], np.float32))
    d["cng%d" % li] = _vec_chunks(np.asarray(inp["cnorm_g"][l], np.float32))
    d["cnb%d" % li] = _vec_chunks(np.asarray(inp["cnorm_b"][l], np.float32))
    fw = np.asarray(inp["ffn_conv_w"][l], np.float32)
    d["ffnw%d" % li] = np.ascontiguousarray(fw.T.reshape(2 * NFF, 128, 3).transpose(1, 0, 2))
    d["ffnb%d" % li] = _vec_chunks(np.asarray(inp["ffn_conv_b"][l], np.float32))
    d["bf%d" % li] = np.ascontiguousarray(np.asarray(inp["b_forget"][l], np.float32).reshape(NH, 1))
    return d


def x_to_dev(x):
    n = x.shape[0]
    return np.ascontiguousarray(x.transpose(0, 2, 1).reshape(n, KD, 128, T))


def x_from_dev(y):
    n = y.shape[0]
    return np.ascontiguousarray(y.reshape(n, D_MODEL, T).transpose(0, 2, 1))


_PROG_CACHE = {}


def get_prog(key):
    if key not in _PROG_CACHE:
        _PROG_CACHE[key] = build_program(*key)
    return _PROG_CACHE[key]


def kernel(**inputs):
    x = np.asarray(inputs["x"], np.float32)
    B = x.shape[0]
    per = B // N_CORES
    consts = make_consts()
    depth = inputs["w_in"].shape[0]
    common = dict(consts)
    common["g_final"] = _vec_chunks(np.asarray(inputs["g_final"], np.float32))
    for l in range(depth):
        common.update(prep_layer(inputs, l, l))
    nc = get_prog((per, depth, True, True, True))
    xd = x_to_dev(x)
    in_maps = []
    for c in range(N_CORES):
        m = dict(common)
        m["xT"] = xd[c * per:(c + 1) * per]
        in_maps.append(m)
    res = run_bass_kernel_spmd(nc, in_maps, core_ids=list(range(N_CORES)))
    y = np.concatenate([np.asarray(r["yT"]) for r in res.results], 0)
    return x_from_dev(y).astype(np.float32)
```

```python
import contextlib
import numpy as np
import concourse.bass as bass
import concourse.mybir as mybir
from concourse.bass_utils import run_bass_kernel_spmd

F32 = mybir.dt.float32
BF16 = mybir.dt.bfloat16
AF = mybir.ActivationFunctionType
ALU = mybir.AluOpType

D_MODEL = 1024
T = 2048
TC = 512
NTC = 4
KD = 8
HD = 64
NH = 6
W_FOX = 384
W_DIL = 384
CONV_CH = 256
CONV_K = 31
D_FF = 2816
NFF = 22
EPS = 1e-6
OFF_QA = 0
OFF_KA = 384
OFF_VA = 768
OFF_FA = 1152
OFF_QB = 1158
OFF_KB = OFF_QB + 384
OFF_VB = OFF_KB + 384
OFF_GV = OFF_VB + 384
OFF_GG = OFF_GV + 256
N_IN = OFF_GG + 256
TTW = 2432
N_CORES = 8

ENGINES = ("pe", "act", "dve", "pool", "sp")


class Op:
    __slots__ = ("eng", "fn", "dma", "deps", "sig", "needed", "gid")

    def __init__(self, eng, fn, dma, gid):
        self.eng = eng
        self.fn = fn
        self.dma = dma
        self.deps = []
        self.sig = None
        self.needed = False
        self.gid = gid


class Sched:
    def __init__(self, nc, n_dma_sems=32):
        self.nc = nc
        self.ops = {e: [] for e in ENGINES}
        self.tok = {}
        self.gid = 0
        self.n_dma_sems = n_dma_sems
        self.all_dma = []

    def op(self, eng, fn, reads=(), writes=(), dma=False, extra_deps=()):
        o = Op(eng, fn, dma, self.gid)
        self.gid += 1
        deps = {}
        for t in reads:
            st = self.tok.get(t)
            if st is not None and st[0] is not None:
                deps[id(st[0])] = st[0]
        for t in writes:
            st = self.tok.get(t)
            if st is not None:
                if st[0] is not None:
                    deps[id(st[0])] = st[0]
                for r in st[1]:
                    deps[id(r)] = r
        for d in extra_deps:
            deps[id(d)] = d
        for t in reads:
            st = self.tok.get(t)
            if st is None:
                st = [None, []]
                self.tok[t] = st
            st[1].append(o)
        for t in writes:
            self.tok[t] = [o, []]
        dl = []
        for d in deps.values():
            if d is o:
                continue
            if eng == "pe" and d.eng == "pe" and not d.dma and not dma:
                continue
            dl.append(d)
        o.deps = dl
        for d in dl:
            d.needed = True
        self.ops[eng].append(o)
        if dma:
            self.all_dma.append(o)
        return o

    def barrier(self):
        lasts = []
        for e in ENGINES:
            for o in reversed(self.ops[e]):
                if not o.dma and o.fn is not None:
                    lasts.append(o)
                    break
        dmas = {}
        for st in self.tok.values():
            if st[0] is not None and st[0].dma:
                dmas[id(st[0])] = st[0]
            for r in st[1]:
                if r.dma:
                    dmas[id(r)] = r
        deps = lasts + list(dmas.values())
        for e in ENGINES:
            if e == "pe":
                self.op(e, None, extra_deps=[d for d in deps])
            else:
                self.op(e, None, extra_deps=deps)

    def emit(self):
        nc = self.nc
        eng_attr = {"pe": "tensor", "act": "scalar", "dve": "vector", "pool": "gpsimd", "sp": "sync"}
        with contextlib.ExitStack() as es:
            esem = {e: es.enter_context(nc.semaphore("s_" + e)) for e in ENGINES}
            dsem = [es.enter_context(nc.semaphore("d%d" % i)) for i in range(self.n_dma_sems)]
            last_on = [None] * self.n_dma_sems
            cnt = [0] * self.n_dma_sems
            key = {}
            val = {}
            for k, o in enumerate(sorted(self.all_dma, key=lambda x: x.gid)):
                s = k % self.n_dma_sems
                cnt[s] += 16
                o.sig = (dsem[s], cnt[s])
                key[id(o)] = ("d", s)
                val[id(o)] = cnt[s]
                if last_on[s] is not None:
                    o.deps.append(last_on[s])
                last_on[s] = o
            for e in ENGINES:
                n = 0
                for o in self.ops[e]:
                    if o.dma or o.fn is None:
                        continue
                    n += 1
                    key[id(o)] = ("e", e)
                    val[id(o)] = n
            allops = sorted([o for e in ENGINES for o in self.ops[e]], key=lambda x: x.gid)
            K = {e: {} for e in ENGINES}
            clock = {}
            waits = {}
            need = set()
            for o in allops:
                Ke = K[o.eng]
                em = []
                for d in o.deps:
                    kd, vd = key[id(d)], val[id(d)]
                    if Ke.get(kd, 0) >= vd:
                        continue
                    em.append(d)
                    need.add(id(d))
                    for k2, v2 in clock[id(d)].items():
                        if Ke.get(k2, 0) < v2:
                            Ke[k2] = v2
                waits[id(o)] = em
                if o.fn is not None:
                    c = dict(Ke)
                    c[key[id(o)]] = val[id(o)]
                    clock[id(o)] = c
            for e in ENGINES:
                c = 0
                for o in self.ops[e]:
                    if o.dma or o.fn is None:
                        continue
                    if id(o) in need:
                        c += 1
                        o.sig = (esem[e], c)
                    else:
                        o.sig = None
            block = es.enter_context(nc.Block())
            for e in ENGINES:
                ops = self.ops[e]
                if not ops:
                    continue

                def body(eng, ops=ops):
                    waited = {}
                    for o in ops:
                        for d in waits[id(o)]:
                            s, v = d.sig
                            if waited.get(id(s), 0) < v:
                                eng.wait_ge(s, v)
                                waited[id(s)] = v
                        if o.fn is None:
                            continue
                        ins = o.fn(eng)
                        if o.dma:
                            ins.then_inc(o.sig[0], 16)
                        elif o.sig is not None:
                            ins.then_inc(o.sig[0], 1)

                getattr(block, eng_attr[e])(body)
            self.stats = (sum(len(w) for w in waits.values()), len(need))


class Arena:
    def __init__(self, nc, limit):
        self.nc = nc
        self.off = 16512
        self.limit = limit
        self.n = 0

    def alloc(self, name, shape, dtype):
        esz = 4 if dtype == F32 else 2
        nb = esz
        for s in shape[1:]:
            nb *= s
        nb = (nb + 63) // 64 * 64
        off = self.off
        self.off += nb
        assert self.off <= self.limit, "SBUF arena overflow at %s: %d > %d" % (name, self.off, self.limit)
        self.n += 1
        return self.nc.alloc_sbuf_tensor_at("%s_%d" % (name, self.n), list(shape), dtype, offset=off)

    def mark(self):
        return self.off

    def reset(self, m):
        self.off = m


def build_program(n_seq, n_layers, do_mixer=True, do_ffn=True, do_final=True):
    nc = bass.Bass("TRN2", target_bir_lowering=False)

    def din(name, shape):
        return nc.dram_tensor(name, list(shape), F32, kind="ExternalInput").ap()

    xT_d = din("xT", [n_seq, KD, 128, T])
    yT_d = nc.dram_tensor("yT", [n_seq, KD, 128, T], F32, kind="ExternalOutput").ap()
    c_ident = din("c_ident", [128, 128])
    c_ones = din("c_ones", [128, 128])
    c_tri = din("c_tri", [128, 128])
    c_ttd = din("c_ttd", [128, TTW])
    gfin_d = din("g_final", [128, KD])
    WL = []
    for l in range(n_layers):
        w = {}
        w["wqkf"] = din("wqkf%d" % l, [NH, 128, KD, 128])
        w["wqkd"] = din("wqkd%d" % l, [3, 2, 128, KD, 128])
        w["wv"] = din("wv%d" % l, [2, 128, KD, 384])
        w["wfa"] = din("wfa%d" % l, [128, KD, NH])
        w["wconv"] = din("wconv%d" % l, [128, KD, 512])
        w["wo"] = din("wo%d" % l, [128, KD, D_MODEL])
        w["wup"] = din("wup%d" % l, [NFF, 128, KD, 256])
        w["wdown"] = din("wdown%d" % l, [128, NFF, D_MODEL])
        w["ln1"] = din("ln1_%d" % l, [128, KD])
        w["ln2"] = din("ln2_%d" % l, [128, KD])
        w["gfox"] = din("gfox%d" % l, [128, 3])
        w["gdil"] = din("gdil%d" % l, [128, 3])
        w["convw"] = din("convw%d" % l, [128, 2, CONV_K])
        w["convb"] = din("convb%d" % l, [128, 2])
        w["cng"] = din("cng%d" % l, [128, 2])
        w["cnb"] = din("cnb%d" % l, [128, 2])
        w["ffnw"] = din("ffnw%d" % l, [128, 2 * NFF, 3])
        w["ffnb"] = din("ffnb%d" % l, [128, 2 * NFF])
        w["bf"] = din("bf%d" % l, [NH, 1])
        WL.append(w)

    S = Sched(nc)
    A = Arena(nc, 229344)
    ps = nc.alloc_psum_tensor("ps", [128, 4096], F32)

    def bank(b, p0=0, p1=128, c0=0, c1=512):
        return ps[p0:p1, b * 512 + c0:b * 512 + c1]

    def PSK(b):
        return ("ps", b)

    xT = A.alloc("xT", [128, KD, T], F32)
    hT = A.alloc("hT", [128, KD, T], BF16)
    ident_f = A.alloc("ident_f", [128, 128], F32)
    ones_f = A.alloc("ones_f", [128, 128], F32)
    ident_b = A.alloc("ident_b", [128, 128], BF16)
    ones_b = A.alloc("ones_b", [128, 128], BF16)
    tri_b = A.alloc("tri_b", [128, 128], BF16)
    ttd_b = A.alloc("ttd_b", [128, TTW], BF16)
    ones512 = A.alloc("ones512", [128, 512], F32)
    gfin = A.alloc("gfin", [128, KD], F32)
    PL = []
    for l in range(n_layers):
        p = {}
        p["ln1"] = A.alloc("ln1", [128, KD], F32)
        p["ln2"] = A.alloc("ln2", [128, KD], F32)
        p["gfox"] = A.alloc("gfox", [128, 3], F32)
        p["gdil"] = A.alloc("gdil", [128, 3], F32)
        p["convw"] = A.alloc("convw", [128, 2, CONV_K], F32)
        p["convb"] = A.alloc("convb", [128, 2], F32)
        p["cng"] = A.alloc("cng", [128, 2], F32)
        p["cnb"] = A.alloc("cnb", [128, 2], F32)
        p["ffnw"] = A.alloc("ffnw", [128, 2 * NFF, 3], F32)
        p["ffnb"] = A.alloc("ffnb", [128, 2 * NFF], F32)
        p["bf"] = A.alloc("bf", [NH, 1], F32)
        p["negb"] = A.alloc("negb", [NH, 1], F32)
        PL.append(p)
    sqring = [A.alloc("sq", [128, TC], BF16) for _ in range(3)]
    rsring = [A.alloc("rs", [128, TC], F32) for _ in range(2)]
    ctr = {"sq": 0, "rs": 0}

    def ld(eng, dst_ap, src_ap, tokw):
        return S.op(eng, lambda e: e.dma_start(out=dst_ap, in_=src_ap), writes=tokw, dma=True)

    ld("sp", ident_f[:], c_ident, ["c_ident_f"])
    ld("sp", ones_f[:], c_ones, ["c_ones_f"])
    ld("pool", ident_b[:], c_ident, ["c_ident_b"])
    ld("pool", ones_b[:], c_ones, ["c_ones_b"])
    ld("pool", tri_b[:], c_tri, ["c_tri"])
    ld("pool", ttd_b[:], c_ttd, ["c_ttd"])
    ld("sp", gfin[:], gfin_d, ["gfin"])
    S.op("dve", lambda e: e.memset(ones512[:], 1.0), writes=["ones512"])
    for l in range(n_layers):
        for k in ("ln1", "ln2", "gfox", "gdil", "convw", "convb", "cng", "cnb", "ffnw", "ffnb", "bf"):
            ld("sp", PL[l][k][:], WL[l][k], [("par", l, k)])
        S.op("dve", lambda e, l=l: e.tensor_scalar(out=PL[l]["negb"][:], in0=PL[l]["bf"][:], scalar1=-1.0,
                                                  scalar2=None, op0=ALU.mult),
             reads=[("par", l, "bf")], writes=[("par", l, "negb")])

    scope_base = A.mark()

    def tsl(t):
        return slice(t * TC, (t + 1) * TC)

    def rmsnorm_to_h(g_tile, gtok, banks):
        for t in range(NTC):
            b = banks[t % len(banks)]
            for c in range(KD):
                i = ctr["sq"] % len(sqring)
                ctr["sq"] += 1
                sq = sqring[i]
                S.op("act", lambda e, sq=sq, c=c, t=t: e.activation(out=sq[:], in_=xT[:, c, tsl(t)], func=AF.Square),
                     reads=[("x", c, t)], writes=[("sq", i)])
                S.op("pe", lambda e, sq=sq, c=c, b=b: e.matmul(bank(b), lhsT=ones_b[:], rhs=sq[:], start=(c == 0),
                                                               stop=(c == KD - 1)),
                     reads=[("sq", i), "c_ones_b"], writes=[PSK(b)])
            r = ctr["rs"] % len(rsring)
            ctr["rs"] += 1
            rs = rsring[r]
            S.op("act", lambda e, rs=rs, b=b: e.activation(out=rs[:], in_=bank(b), func=AF.Ln, scale=1.0 / D_MODEL, bias=EPS),
                 reads=[PSK(b)], writes=[("rs", r)])
            S.op("act", lambda e, rs=rs: e.activation(out=rs[:], in_=rs[:], func=AF.Exp, scale=-0.5),
                 reads=[("rs", r)], writes=[("rs", r)])
            for c in range(KD):
                S.op("dve", lambda e, rs=rs, c=c, t=t: e.scalar_tensor_tensor(out=hT[:, c, tsl(t)], in0=xT[:, c, tsl(t)],
                                                                           scalar=g_tile[:, c:c + 1], in1=rs[:],
                                                                           op0=ALU.mult, op1=ALU.mult),
                     reads=[("x", c, t), ("rs", r), gtok], writes=[("h", c, t)])

    def proj(wfn, wtoks, M, t, b, ncols=TC):
        for k in range(KD):
            S.op("pe", lambda e, k=k: e.matmul(bank(b, 0, M), lhsT=wfn(k), rhs=hT[:, k, tsl(t)], start=(k == 0),
                                               stop=(k == KD - 1)),
                 reads=[("h", k, t)] + wtoks, writes=[PSK(b)])

    def group_norm_wo(l, yTg, npc, gtile, gtok, width, wo_chunk0, banks, wo_t):
        if gtile is not None:
            for t in range(NTC):
                b = banks[t % len(banks)]
                for pc in range(npc):
                    i = ctr["sq"] % len(sqring)
                    ctr["sq"] += 1
                    sq = sqring[i]
                    S.op("act", lambda e, sq=sq, pc=pc, t=t: e.activation(out=sq[:], in_=yTg[:, pc, tsl(t)], func=AF.Square),
                         reads=[("yg", pc, t, 0), ("yg", pc, t, 1)], writes=[("sq", i)])
                    S.op("pe", lambda e, sq=sq, pc=pc, b=b: e.matmul(bank(b), lhsT=ones_b[:], rhs=sq[:], start=(pc == 0),
                                                                     stop=(pc == npc - 1)),
                         reads=[("sq", i), "c_ones_b"], writes=[PSK(b)])
                r = ctr["rs"] % len(rsring)
                ctr["rs"] += 1
                rs = rsring[r]
                S.op("act", lambda e, rs=rs, b=b: e.activation(out=rs[:], in_=bank(b), func=AF.Ln, scale=1.0 / width, bias=EPS),
                     reads=[PSK(b)], writes=[("rs", r)])
                S.op("act", lambda e, rs=rs: e.activation(out=rs[:], in_=rs[:], func=AF.Exp, scale=-0.5),
                     reads=[("rs", r)], writes=[("rs", r)])
                for pc in range(npc):
                    S.op("dve", lambda e, rs=rs, pc=pc, t=t: e.scalar_tensor_tensor(
                        out=yTg[:, pc, tsl(t)], in0=yTg[:, pc, tsl(t)], scalar=gtile[:, pc:pc + 1], in1=rs[:],
                        op0=ALU.mult, op1=ALU.mult),
                         reads=[("yg", pc, t, 0), ("yg", pc, t, 1), ("rs", r), gtok],
                         writes=[("yg", pc, t, 0), ("yg", pc, t, 1)])
        ld("pool", wo_t[:, 0:npc, :], WL[l]["wo"][:, wo_chunk0:wo_chunk0 + npc, :], ["wo"])
        n = 0
        for e_ in range(KD):
            for t in range(NTC):
                b = banks[n % len(banks)]
                n += 1
                for pc in range(npc):
                    S.op("pe", lambda e, pc=pc, e_=e_, t=t, b=b: e.matmul(bank(b), lhsT=wo_t[:, pc, e_ * 128:(e_ + 1) * 128],
                                                                       rhs=yTg[:, pc, tsl(t)], start=(pc == 0),
                                                                       stop=(pc == npc - 1)),
                         reads=[("yg", pc, t, 0), ("yg", pc, t, 1), "wo"], writes=[PSK(b)])
                S.op("dve", lambda e, e_=e_, t=t, b=b: e.tensor_tensor(out=xT[:, e_, tsl(t)], in0=bank(b), in1=xT[:, e_, tsl(t)],
                                                                   op=ALU.add),
                     reads=[PSK(b), ("x", e_, t)], writes=[("x", e_, t)])

    def conv_branch(l):
        A.reset(scope_base)
        P = PL[l]
        wconv = A.alloc("wconv", [128, KD, 512], BF16)
        ycpad = A.alloc("ycpad", [128, 2, 30 + T], BF16)
        dg = A.alloc("dg", [128, 2, CONV_K, 128], BF16)
        yTg = A.alloc("yTgc", [128, 2, T], BF16)
        wo_t = A.alloc("wo_c", [128, 2, D_MODEL], BF16)
        sg = [A.alloc("sg", [128, TC], F32) for _ in range(2)]
        u = [[A.alloc("u", [128, TC], F32) for _ in range(2)] for _ in range(2)]
        dd = [[A.alloc("dd", [128, TC], F32) for _ in range(2)] for _ in range(2)]
        sq2 = [[A.alloc("sq2", [128, TC], F32) for _ in range(2)] for _ in range(2)]
        ld("pool", wconv[:], WL[l]["wconv"], ["wconv"])
        S.op("dve", lambda e: e.memset(ycpad[:, :, 0:30], 0.0), writes=[("ycpad", 0), ("ycpad", 1)])
        for cc in range(2):
            for k in range(CONV_K):
                S.op("dve", lambda e, cc=cc, k=k: e.tensor_scalar(out=dg[:, cc, k, :], in0=ident_b[:],
                                                                 scalar1=P["convw"][:, cc, k:k + 1], scalar2=None,
                                                                 op0=ALU.mult),
                     reads=["c_ident_b", ("par", l, "convw")], writes=[("dg", cc, k)])
        pb = [0, 1, 2, 3]
        n = 0
        for cc in range(2):
            for t in range(NTC):
                ba = pb[n % 4]
                bb = pb[(n + 1) % 4]
                n += 2
                proj(lambda k, cc=cc: wconv[:, k, cc * 128:(cc + 1) * 128], ["wconv"], 128, t, ba)
                proj(lambda k, cc=cc: wconv[:, k, 256 + cc * 128:256 + (cc + 1) * 128], ["wconv"], 128, t, bb)
                si = (n // 2) % 2
                S.op("act", lambda e, si=si, bb=bb: e.activation(out=sg[si][:], in_=bank(bb), func=AF.Sigmoid),
                     reads=[PSK(bb)], writes=[("sg", si)])
                S.op("dve", lambda e, si=si, ba=ba, cc=cc, t=t: e.tensor_tensor(
                    out=ycpad[:, cc, 30 + t * TC:30 + (t + 1) * TC], in0=bank(ba), in1=sg[si][:], op=ALU.mult),
                     reads=[PSK(ba), ("sg", si)], writes=[("yc", cc, t)])
        for t in range(NTC):
            r2 = t % 2
            for cc in range(2):
                b = pb[(2 * t + cc) % 4]
                for k in range(CONV_K):
                    rd = [("dg", cc, k), ("yc", cc, t), ("ycpad", cc)]
                    if t > 0:
                        rd.append(("yc", cc, t - 1))
                    S.op("pe", lambda e, cc=cc, k=k, t=t, b=b: e.matmul(bank(b), lhsT=dg[:, cc, k, :],
                                                                     rhs=ycpad[:, cc, t * TC + k:t * TC + k + TC],
                                                                     start=(k == 0), stop=(k == CONV_K - 1)),
                         reads=rd, writes=[PSK(b)])
                S.op("act", lambda e, cc=cc, b=b, r2=r2: e.activation(out=u[cc][r2][:], in_=bank(b), func=AF.Identity,
                                                                   bias=P["convb"][:, cc:cc + 1], scale=1.0),
                     reads=[PSK(b), ("par", l, "convb")], writes=[("u", cc, r2)])
            bs = 4 + (t % 2) * 2
            for cc in range(2):
                S.op("pe", lambda e, cc=cc, r2=r2, bs=bs: e.matmul(bank(bs), lhsT=ones_f[:], rhs=u[cc][r2][:], start=(cc == 0),
                                                                stop=(cc == 1)),
                     reads=[("u", cc, r2), "c_ones_f"], writes=[PSK(bs)])
            for cc in range(2):
                S.op("dve", lambda e, cc=cc, r2=r2, bs=bs: e.scalar_tensor_tensor(
                    out=dd[cc][r2][:], in0=bank(bs), scalar=-1.0 / CONV_CH, in1=u[cc][r2][:], op0=ALU.mult, op1=ALU.add),
                     reads=[PSK(bs), ("u", cc, r2)], writes=[("dd", cc, r2)])
                S.op("act", lambda e, cc=cc, r2=r2: e.activation(out=sq2[cc][r2][:], in_=dd[cc][r2][:], func=AF.Square),
                     reads=[("dd", cc, r2)], writes=[("sq2", cc, r2)])
            for cc in range(2):
                S.op("pe", lambda e, cc=cc, r2=r2, bs=bs: e.matmul(bank(bs + 1), lhsT=ones_f[:], rhs=sq2[cc][r2][:],
                                                                start=(cc == 0), stop=(cc == 1)),
                     reads=[("sq2", cc, r2), "c_ones_f"], writes=[PSK(bs + 1)])
            r = ctr["rs"] % len(rsring)
            ctr["rs"] += 1
            rs = rsring[r]
            S.op("act", lambda e, rs=rs, bs=bs: e.activation(out=rs[:], in_=bank(bs + 1), func=AF.Ln, scale=1.0 / CONV_CH, bias=EPS),
                 reads=[PSK(bs + 1)], writes=[("rs", r)])
            S.op("act", lambda e, rs=rs: e.activation(out=rs[:], in_=rs[:], func=AF.Exp, scale=-0.5),
                 reads=[("rs", r)], writes=[("rs", r)])
            for cc in range(2):
                S.op("dve", lambda e, cc=cc, r2=r2, rs=rs: e.tensor_tensor(out=dd[cc][r2][:], in0=dd[cc][r2][:], in1=rs[:],
                                                                        op=ALU.mult),
                     reads=[("dd", cc, r2), ("rs", r)], writes=[("dd", cc, r2)])
                S.op("act", lambda e, cc=cc, r2=r2, t=t: e.activation(out=yTg[:, cc, tsl(t)], in_=dd[cc][r2][:], func=AF.Silu,
                                                                   scale=P["cng"][:, cc:cc + 1], bias=P["cnb"][:, cc:cc + 1]),
                     reads=[("dd", cc, r2), ("par", l, "cng"), ("par", l, "cnb")],
                     writes=[("yg", cc, t, 0), ("yg", cc, t, 1)])
        group_norm_wo(l, yTg, 2, None, None, None, 6, [0, 1, 2, 3], wo_t)
        S.barrier()

    def attn_group(l, kind):
        A.reset(scope_base)
        P = PL[l]
        fox = kind == "fox"
        KQ = 67
        yTg = A.alloc("yTg", [128, 3, T], BF16)
        m_wo = A.mark()
        wo_t = A.alloc("wo_a", [128, 3, D_MODEL], BF16)
        A.reset(m_wo)
        if fox:
            qaug = [A.alloc("qaug", [128, T], BF16) for _ in range(2)]
            kaug = [A.alloc("kaug", [128, T], BF16) for _ in range(2)]
            wqk = [A.alloc("wqk", [128, KD, 128], BF16) for _ in range(2)]
            kst = [A.alloc("kst", [128, TC], BF16) for _ in range(2)]
        else:
            qaug = [A.alloc("qpair", [128, T], BF16) for _ in range(2)]
            kpad = [[A.alloc("kpad", [128, T], BF16) for _ in range(2)] for _ in range(2)]
            wqk = [[A.alloc("wqk", [128, KD, 128], BF16) for _ in range(2)] for _ in range(2)]
        wv = A.alloc("wv", [128, KD, 384], BF16)
        NPT = 7
        pT = [A.alloc("pT", [128, TC], BF16) for _ in range(NPT)]
        numsb = [A.alloc("numsb", [128, TC], F32) for _ in range(2)]
        bc = [A.alloc("bc", [128, TC], F32) for _ in range(2)]
        if fox:
            dkT = A.alloc("dkT", [128, 16 * NH], F32)
            csplit = A.alloc("csplit", [NH, 3, T], BF16)
            m2 = A.mark()
            wfa = A.alloc("wfa", [128, KD, NH], BF16)
            dT = A.alloc("dT", [NH, T], F32)
            e6 = [A.alloc("e6", [NH, TC], F32) for _ in range(2)]
            tA = A.alloc("tA", [NH, TC], F32)
            tB = A.alloc("tB", [NH, TC], F32)
            A.reset(m2)
        vaug = A.alloc("vaug", [128, 16, NH, 128], BF16)
        PB = [5, 6, 7]
        pbn = [0]

        def nextpb():
            b = PB[pbn[0] % len(PB)]
            pbn[0] += 1
            return b

        grp = 0 if fox else 1
        ld("pool", wv[:], WL[l]["wv"][grp], ["wv"])
        for s_ in range(2):
            if fox:
                S.op("dve", lambda e, s_=s_: e.memset(kaug[s_][64:67, :], 1.0), writes=[("kones", s_)])
            else:
                S.op("dve", lambda e, s_=s_: e.memset(kpad[s_][0][64:128, :], 0.0), writes=[("kz", s_, 0)])
                S.op("dve", lambda e, s_=s_: e.memset(kpad[s_][1][0:64, :], 0.0), writes=[("kz", s_, 1)])
        if fox:
            ld("pool", wfa[:], WL[l]["wfa"], ["wfa"])
            for t in range(NTC):
                b = nextpb()
                proj(lambda k: wfa[:, k, :], ["wfa"], NH, t, b)
                ei = t % 2
                S.op("act", lambda e, b=b, ei=ei: e.activation(out=e6[ei][:], in_=bank(b, 0, NH), func=AF.Exp, scale=-1.0,
                                                            bias=P["negb"][:, 0:1]),
                     reads=[PSK(b), ("par", l, "negb")], writes=[("e6", ei)])
                S.op("act", lambda e, ei=ei: e.activation(out=e6[ei][:], in_=e6[ei][:], func=AF.Ln, scale=1.0, bias=1.0),
                     reads=[("e6", ei)], writes=[("e6", ei)])
                init = 0.0 if t == 0 else dT[:, t * TC - 1:t * TC]
                rd = [("e6", ei), "ones512"] + ([("dT", t - 1)] if t > 0 else [])
                S.op("dve", lambda e, ei=ei, t=t, init=init: e.tensor_tensor_scan(
                    out=dT[:, tsl(t)], data0=ones512[0:NH, :], data1=e6[ei][:], initial=init, op0=ALU.mult, op1=ALU.add),
                     reads=rd, writes=[("dT", t)])
                S.op("dve", lambda e, t=t: e.tensor_scalar(out=tA[:], in0=dT[:, tsl(t)], scalar1=-8.0, scalar2=None, op0=ALU.mult),
                     reads=[("dT", t)], writes=["tA"])
                S.op("dve", lambda e, t=t: e.tensor_copy(out=csplit[:, 0, tsl(t)], in_=tA[:]), reads=["tA"], writes=[("cs", 0, t)])
                S.op("dve", lambda e, t=t: e.tensor_tensor(out=tB[:], in0=tA[:], in1=csplit[:, 0, tsl(t)], op=ALU.subtract),
                     reads=["tA", ("cs", 0, t)], writes=["tB"])
                S.op("dve", lambda e, t=t: e.tensor_copy(out=csplit[:, 1, tsl(t)], in_=tB[:]), reads=["tB"], writes=[("cs", 1, t)])
                S.op("dve", lambda e, t=t: e.tensor_tensor(out=tA[:], in0=tB[:], in1=csplit[:, 1, tsl(t)], op=ALU.subtract),
                     reads=["tB", ("cs", 1, t)], writes=["tA"])
                S.op("dve", lambda e, t=t: e.tensor_copy(out=csplit[:, 2, tsl(t)], in_=tA[:]), reads=["tA"], writes=[("cs", 2, t)])
            bt = nextpb()
            for j in range(16):
                S.op("pe", lambda e, j=j, bt=bt: e.transpose(out=bank(bt, 0, 128, j * NH, (j + 1) * NH),
                                                            in_=dT[:, j * 128:(j + 1) * 128], identity=ident_f[0:NH, 0:NH]),
                     reads=[("dT", j // 4), "c_ident_f"], writes=[PSK(bt)])
            S.op("dve", lambda e, bt=bt: e.tensor_copy(out=dkT[:], in_=bank(bt, 0, 128, 0, 16 * NH)),
                 reads=[PSK(bt)], writes=["dkT"])
            S.barrier()
        S.op("dve", lambda e: e.memset(vaug[:, :, 0:NH:2, 64:128], 1.0), writes=["vones"])
        S.op("dve", lambda e: e.memset(vaug[:, :, 1:NH:2, 0:64], 1.0), writes=["vones2"])
        for j in range(16):
            b = nextpb()
            for k in range(KD):
                S.op("pe", lambda e, j=j, k=k, b=b: e.matmul(bank(b, 0, 128, 0, 384), lhsT=hT[:, k, j * 128:(j + 1) * 128],
                                                           rhs=wv[:, k, :], start=(k == 0), stop=(k == KD - 1)),
                     reads=[("h", k, j // 4), "wv"], writes=[PSK(b)])
            S.op("act", lambda e, j=j, b=b: e.activation(
                out=vaug[:, j, 0:NH:2, 0:64],
                in_=bank(b, 0, 128, 0, 384).rearrange("p (e two d) -> p e two d", two=2, d=HD)[:, :, 0, :], func=AF.Copy),
                 reads=[PSK(b)], writes=[("v", j, 0)])
            S.op("act", lambda e, j=j, b=b: e.activation(
                out=vaug[:, j, 1:NH:2, 64:128],
                in_=bank(b, 0, 128, 0, 384).rearrange("p (e two d) -> p e two d", two=2, d=HD)[:, :, 1, :], func=AF.Copy),
                 reads=[PSK(b)], writes=[("v", j, 1)])

        wsrc = WL[l]["wqkf"] if fox else WL[l]["wqkd"]
        SB = [0, 1, 2, 3, 4]
        OB = [5, 6]
        nblk = [0]
        nnorm = [0]
        nkst = [0]
        pending = []
        units = [[h] for h in range(NH)] if fox else [[0, 1], [2, 3], [4, 5]]

        def load_unit(u):
            s = u % 2
            if fox:
                ld("pool", wqk[s][:], wsrc[u], [("wqk", s)])
            else:
                ld("pool", wqk[s][0][:], wsrc[u, 0], [("wqk", s, 0)])
                ld("pool", wqk[s][1][:], wsrc[u, 1], [("wqk", s, 1)])

        def make_proj_items(u):
            s = u % 2
            items = []
            for t in range(NTC):
                for which in ((0,) if fox else (1, 0)):
                    bref = {}
                    for k in range(KD):
                        def mm(k=k, t=t, which=which, bref=bref, s=s):
                            if k == 0:
                                bref["b"] = nextpb()
                            b = bref["b"]
                            if fox:
                                wap, wtok = wqk[s][:, k, :], ("wqk", s)
                            else:
                                wap, wtok = wqk[s][which][:, k, :], ("wqk", s, which)
                            S.op("pe", lambda e: e.matmul(bank(b), lhsT=wap, rhs=hT[:, k, tsl(t)],
                                                           start=(k == 0), stop=(k == KD - 1)),
                                 reads=[("h", k, t), wtok], writes=[PSK(b)])
                        items.append(mm)
                    if fox:
                        def evq(t=t, bref=bref, s=s):
                            b = bref["b"]
                            S.op("dve", lambda e: e.tensor_copy(out=qaug[s][0:HD, tsl(t)], in_=bank(b, 0, HD)),
                                 reads=[PSK(b)], writes=[("q", s, t)])

                        def evk(t=t, bref=bref, s=s):
                            b = bref["b"]
                            r = nkst[0] % 2
                            nkst[0] += 1
                            bref["r"] = r
                            S.op("dve", lambda e: e.tensor_copy(out=kst[r][64:128, :], in_=bank(b, 64, 128)),
                                 reads=[PSK(b)], writes=[("kst", r)])

                        def evd(t=t, bref=bref, s=s):
                            r = bref["r"]
                            S.op("sp", lambda e: e.dma_start(out=kaug[s][0:HD, tsl(t)], in_=kst[r][64:128, :]),
                                 reads=[("kst", r)], writes=[("k", s, t)], dma=True)
                        items += [evq, evk, evd]
                    elif which == 0:
                        def evq(t=t, bref=bref, s=s):
                            b = bref["b"]
                            S.op("dve", lambda e: e.tensor_copy(out=qaug[s][:, tsl(t)], in_=bank(b)),
                                 reads=[PSK(b)], writes=[("q", s, t)])
                        items.append(evq)
                    else:
                        def evk0(t=t, bref=bref, s=s):
                            b = bref["b"]
                            S.op("act", lambda e: e.activation(out=kpad[s][0][0:64, tsl(t)], in_=bank(b, 0, 64), func=AF.Copy),
                                 reads=[PSK(b)], writes=[("k", s, t, 0)])

                        def evk1(t=t, bref=bref, s=s):
                            b = bref["b"]
                            S.op("act", lambda e: e.activation(out=kpad[s][1][64:128, tsl(t)], in_=bank(b, 64, 128), func=AF.Copy),
                                 reads=[PSK(b)], writes=[("k", s, t, 1)])
                        items += [evk0, evk1]
            return items

        def crow_dmas(h):
            s = h % 2
            for i in range(3):
                S.op("sp", lambda e, s=s, h=h, i=i: e.dma_start(out=qaug[s][64 + i:65 + i, :], in_=csplit[h:h + 1, i, :]),
                     reads=[("cs", i, t) for t in range(NTC)], writes=[("qc", s, i)], dma=True)

        load_unit(0)
        for it_ in make_proj_items(0):
            it_()
        if fox:
            crow_dmas(0)
        PB[:] = [7]
        for u, heads in enumerate(units):
          s = u % 2
          nxt = []
          if u + 1 < len(units):
              load_unit(u + 1)
              nxt = make_proj_items(u + 1)
          rate = 2 if fox else 1
          for h in heads:
            pc = h // 2
            half = h % 2
            blocks = [(c, j) for c in range(NTC) for j in range(4 * c + 4)]
            LA = 4
            nb = len(blocks)
            info = {}
            for i in range(nb + LA):
                if i < nb:
                    c, j = blocks[i]
                    m = j - 4 * c if j >= 4 * c else 0
                    col0 = 128 * m
                    sb = SB[nblk[0] % 5]
                    pi = nblk[0] % NPT
                    nblk[0] += 1
                    info[i] = (c, j, col0, sb, pi)
                    if fox:
                        rd = [("k", s, j // 4), ("q", s, c), ("kones", s)] + [("qc", s, ii) for ii in range(3)]
                        S.op("pe", lambda e, s=s, c=c, j=j, col0=col0, sb=sb: e.matmul(
                            bank(sb, 0, 128, col0, TC), lhsT=kaug[s][0:KQ, j * 128:(j + 1) * 128],
                            rhs=qaug[s][0:KQ, c * TC + col0:(c + 1) * TC], start=True, stop=True),
                             reads=rd, writes=[PSK(sb)])
                    else:
                        rd = [("k", s, j // 4, half), ("kz", s, half), ("q", s, c)]
                        S.op("pe", lambda e, s=s, c=c, j=j, col0=col0, sb=sb, half=half: e.matmul(
                            bank(sb, 0, 128, col0, TC), lhsT=kpad[s][half][:, j * 128:(j + 1) * 128],
                            rhs=qaug[s][:, c * TC + col0:(c + 1) * TC], start=True, stop=True),
                             reads=rd, writes=[PSK(sb)])
                    if fox:
                        S.op("act", lambda e, sb=sb, pi=pi, col0=col0, j=j, h=h: e.activation(
                            out=pT[pi][:, col0:TC], in_=bank(sb, 0, 128, col0, TC), func=AF.Exp, scale=0.125,
                            bias=dkT[:, j * NH + h:j * NH + h + 1]),
                             reads=[PSK(sb), "dkT"], writes=[("pT", pi)])
                        if j >= 4 * c:
                            S.op("dve", lambda e, pi=pi, col0=col0: e.tensor_tensor(
                                out=pT[pi][:, col0:col0 + 128], in0=pT[pi][:, col0:col0 + 128], in1=tri_b[:], op=ALU.mult),
                                 reads=[("pT", pi), "c_tri"], writes=[("pT", pi)])
                    else:
                        S.op("act", lambda e, sb=sb, pi=pi, col0=col0: e.activation(
                            out=pT[pi][:, col0:TC], in_=bank(sb, 0, 128, col0, TC), func=AF.Exp, scale=0.125),
                             reads=[PSK(sb)], writes=[("pT", pi)])
                        x0 = 512 * c - 128 * j + 384 + col0
                        S.op("dve" if (nblk[0] % 2 == 0) else "pool", lambda e, pi=pi, col0=col0, x0=x0: e.tensor_tensor(
                            out=pT[pi][:, col0:TC], in0=pT[pi][:, col0:TC], in1=ttd_b[:, x0:x0 + TC - col0], op=ALU.mult),
                             reads=[("pT", pi), "c_ttd"], writes=[("pT", pi)])
                if i >= LA:
                    c, j, col0, sb, pi = info[i - LA]
                    ob = OB[c % 2]
                    last = (j == 4 * c + 3)
                    S.op("pe", lambda e, j=j, h=h, col0=col0, pi=pi, ob=ob, last=last: e.matmul(
                        bank(ob, 0, 128, col0, TC), lhsT=vaug[:, j, h, :], rhs=pT[pi][:, col0:TC], start=(j == 0), stop=last),
                         reads=[("pT", pi), ("v", j, half), "vones", "vones2"], writes=[PSK(ob)])
                    if last:
                        ni = nnorm[0] % 2
                        nnorm[0] += 1
                        rn = slice(0, 64) if half == 0 else slice(64, 128)
                        rl = slice(64, 128) if half == 0 else slice(0, 64)
                        S.op("act", lambda e, ni=ni, ob=ob: e.activation(out=numsb[ni][:], in_=bank(ob), func=AF.Copy),
                             reads=[PSK(ob)], writes=[("num", ni)])
                        S.op("sp", lambda e, ni=ni, rn=rn, rl=rl: e.dma_start(out=bc[ni][rn, :], in_=numsb[ni][rl, :]),
                             reads=[("num", ni)], writes=[("bc", ni)], dma=True)

                        def fin(ni=ni, rn=rn, pc=pc, c=c, half=half):
                            S.op("dve", lambda e: e.reciprocal(out=bc[ni][rn, :], in_=bc[ni][rn, :]),
                                 reads=[("bc", ni)], writes=[("bc", ni)])
                            S.op("dve", lambda e: e.tensor_tensor(
                                out=yTg[rn, pc, tsl(c)], in0=numsb[ni][rn, :], in1=bc[ni][rn, :], op=ALU.mult),
                                 reads=[("num", ni), ("bc", ni)], writes=[("yg", pc, c, half)])
                        pending.append([nblk[0] + 6, fin])
                if pending and nblk[0] >= pending[0][0]:
                    pending.pop(0)[1]()
                for _ in range(rate):
                    if nxt:
                        nxt.pop(0)()
          while nxt:
              nxt.pop(0)()
          if fox and u + 1 < len(units):
              crow_dmas(u + 1)
        while pending:
            pending.pop(0)[1]()
        gt = P["gfox"] if fox else P["gdil"]
        S.barrier()
        group_norm_wo(l, yTg, 3, gt, ("par", l, "gfox" if fox else "gdil"), 384, 0 if fox else 3, [6, 7, 0, 1], wo_t)
        S.barrier()

    def ffn(l):
        A.reset(scope_base)
        P = PL[l]
        G = 6
        wup = [A.alloc("wup", [128, KD, 256], BF16) for _ in range(3)]
        wd = A.alloc("wd", [128, G, D_MODEL], BF16)
        hid = [A.alloc("hid", [128, T], BF16) for _ in range(G)]
        At = [A.alloc("At", [128, T], F32) for _ in range(2)]
        Bt = [A.alloc("Bt", [128, T], F32) for _ in range(2)]
        Gt = [A.alloc("Gt", [128, T], BF16) for _ in range(2)]
        rmsnorm_to_h(P["ln2"], ("par", l, "ln2"), [0, 1, 2, 3])
        groups = []
        i0 = 0
        while i0 < NFF:
            groups.append(list(range(i0, min(NFF, i0 + G))))
            i0 += G
        nw = 0
        for grp in groups:
            for il, i in enumerate(grp):
                ws = nw % 3
                r2 = nw % 2
                nw += 1
                ld("pool", wup[ws][:], WL[l]["wup"][i], [("wup", ws)])
                for half in range(2):
                    for t in range(NTC):
                        b = half * 4 + t
                        proj(lambda k, ws=ws, half=half: wup[ws][:, k, half * 128:(half + 1) * 128], [("wup", ws)], 128, t, b)
                for half in range(2):
                    fi = half * NFF + i
                    dst = At[r2] if half == 0 else Bt[r2]
                    dtok = ("At", r2) if half == 0 else ("Bt", r2)
                    pbk = [PSK(half * 4 + t) for t in range(NTC)]
                    base = half * 2048
                    S.op("act", lambda e, dst=dst, base=base, fi=fi: e.activation(
                        out=dst[:], in_=ps[:, base:base + T], func=AF.Identity, scale=P["ffnw"][:, fi, 2:3],
                        bias=P["ffnb"][:, fi:fi + 1]),
                         reads=pbk + [("par", l, "ffnw"), ("par", l, "ffnb")], writes=[dtok])
                    S.op("dve", lambda e, dst=dst, base=base, fi=fi: e.scalar_tensor_tensor(
                        out=dst[:, 1:T], in0=ps[:, base:base + T - 1], scalar=P["ffnw"][:, fi, 1:2], in1=dst[:, 1:T],
                        op0=ALU.mult, op1=ALU.add),
                         reads=pbk + [dtok, ("par", l, "ffnw")], writes=[dtok])
                    S.op("dve", lambda e, dst=dst, base=base, fi=fi: e.scalar_tensor_tensor(
                        out=dst[:, 2:T], in0=ps[:, base:base + T - 2], scalar=P["ffnw"][:, fi, 0:1], in1=dst[:, 2:T],
                        op0=ALU.mult, op1=ALU.add),
                         reads=pbk + [dtok, ("par", l, "ffnw")], writes=[dtok])
                    if half == 0:
                        S.op("act", lambda e, r2=r2: e.activation(out=Gt[r2][:], in_=At[r2][:], func=AF.Silu),
                             reads=[("At", r2)], writes=[("Gt", r2)])
                S.op("dve", lambda e, r2=r2, il=il: e.tensor_tensor(out=hid[il][:], in0=Gt[r2][:], in1=Bt[r2][:], op=ALU.mult),
                     reads=[("Gt", r2), ("Bt", r2)], writes=[("hid", il)])
            ng = len(grp)
            ld("pool", wd[:, 0:ng, :], WL[l]["wdown"][:, grp[0]:grp[0] + ng, :], ["wd"])
            n = 0
            for e_ in range(KD):
                for t in range(NTC):
                    b = n % 8
                    n += 1
                    for il in range(ng):
                        S.op("pe", lambda e, il=il, e_=e_, t=t, b=b: e.matmul(bank(b), lhsT=wd[:, il, e_ * 128:(e_ + 1) * 128],
                                                                           rhs=hid[il][:, tsl(t)], start=(il == 0),
                                                                           stop=(il == ng - 1)),
                             reads=[("hid", il), "wd"], writes=[PSK(b)])
                    S.op("dve", lambda e, e_=e_, t=t, b=b: e.tensor_tensor(out=xT[:, e_, tsl(t)], in0=bank(b), in1=xT[:, e_, tsl(t)],
                                                                       op=ALU.add),
                         reads=[PSK(b), ("x", e_, t)], writes=[("x", e_, t)])
        S.barrier()

    def final_store(s):
        A.reset(scope_base)
        outst = [A.alloc("outst", [128, TC], F32) for _ in range(4)]
        outs = []
        n = 0
        for t in range(NTC):
            b = t % 4
            for c in range(KD):
                i = ctr["sq"] % len(sqring)
                ctr["sq"] += 1
                sq = sqring[i]
                S.op("act", lambda e, sq=sq, c=c, t=t: e.activation(out=sq[:], in_=xT[:, c, tsl(t)], func=AF.Square),
                     reads=[("x", c, t)], writes=[("sq", i)])
                S.op("pe", lambda e, sq=sq, c=c, b=b: e.matmul(bank(b), lhsT=ones_b[:], rhs=sq[:], start=(c == 0), stop=(c == KD - 1)),
                     reads=[("sq", i), "c_ones_b"], writes=[PSK(b)])
            r = ctr["rs"] % len(rsring)
            ctr["rs"] += 1
            rs = rsring[r]
            S.op("act", lambda e, rs=rs, b=b: e.activation(out=rs[:], in_=bank(b), func=AF.Ln, scale=1.0 / D_MODEL, bias=EPS),
                 reads=[PSK(b)], writes=[("rs", r)])
            S.op("act", lambda e, rs=rs: e.activation(out=rs[:], in_=rs[:], func=AF.Exp, scale=-0.5),
                 reads=[("rs", r)], writes=[("rs", r)])
            for c in range(KD):
                oi = n % 4
                n += 1
                S.op("dve", lambda e, rs=rs, c=c, t=t, oi=oi: e.scalar_tensor_tensor(
                    out=outst[oi][:], in0=xT[:, c, tsl(t)], scalar=gfin[:, c:c + 1], in1=rs[:], op0=ALU.mult, op1=ALU.mult),
                     reads=[("x", c, t), ("rs", r), "gfin"], writes=[("outst", oi)])
                outs.append(S.op("sp", lambda e, c=c, t=t, oi=oi: e.dma_start(out=yT_d[s, c, :, tsl(t)], in_=outst[oi][:]),
                                 reads=[("outst", oi)], dma=True))
        S.barrier()
        return outs

    def raw_store(s):
        outs = []
        for c in range(KD):
            outs.append(S.op("sp", lambda e, c=c: e.dma_start(out=yT_d[s, c, :, :], in_=xT[:, c, :]),
                             reads=[("x", c, t) for t in range(NTC)], dma=True))
        return outs

    all_outs = []
    for s in range(n_seq):
        for c in range(KD):
            S.op("sp", lambda e, c=c, s=s: e.dma_start(out=xT[:, c, :], in_=xT_d[s, c, :, :]),
                 writes=[("x", c, t) for t in range(NTC)], dma=True)
        for l in range(n_layers):
            if do_mixer:
                A.reset(scope_base)
                rmsnorm_to_h(PL[l]["ln1"], ("par", l, "ln1"), [0, 1, 2, 3])
                conv_branch(l)
                attn_group(l, "fox")
                attn_group(l, "dil")
            if do_ffn:
                ffn(l)
        if do_final:
            all_outs += final_store(s)
        else:
            all_outs += raw_store(s)
    S.op("sp", None, extra_deps=all_outs)
    S.emit()
    return nc


def _chunk_rows(w):
    kk = w.shape[0] // 128
    return np.ascontiguousarray(w.reshape(kk, 128, w.shape[1]).transpose(1, 0, 2))


def _vec_chunks(v):
    kk = v.shape[0] // 128
    return np.ascontiguousarray(v.reshape(kk, 128).T)


def make_consts():
    p = np.arange(128)[:, None]
    f = np.arange(128)[None, :]
    ident = (p == f).astype(np.float32)
    ones = np.ones((128, 128), np.float32)
    tri = (f >= p).astype(np.float32)
    x = np.arange(TTW)[None, :]
    d = x - 384 - p
    m1 = (d >= 0) & (d <= 128)
    m2 = (d >= 0) & (d % 4 == 0) & (d <= 512)
    m3 = (d >= 0) & (d % 16 == 0) & (d <= 2048)
    ttd = (m1.astype(np.float32) + m2.astype(np.float32) + m3.astype(np.float32))
    return {"c_ident": ident, "c_ones": ones, "c_tri": tri, "c_ttd": np.ascontiguousarray(ttd)}


def prep_layer(inp, l, li):
    w_in = np.asarray(inp["w_in"][l], np.float32)
    wr = _chunk_rows(w_in)
    d = {}

    def heads(off):
        return [np.ascontiguousarray(wr[:, :, off + h * HD:off + (h + 1) * HD]) for h in range(NH)]

    qa, ka, qb, kb = heads(OFF_QA), heads(OFF_KA), heads(OFF_QB), heads(OFF_KB)
    d["wqkf%d" % li] = np.stack([np.concatenate([qa[h], ka[h]], 2) for h in range(NH)], 0)
    d["wqkd%d" % li] = np.stack([np.stack([np.concatenate([qb[2 * p_], qb[2 * p_ + 1]], 2),
                                           np.concatenate([kb[2 * p_], kb[2 * p_ + 1]], 2)], 0) for p_ in range(3)], 0)
    d["wv%d" % li] = np.stack([np.ascontiguousarray(wr[:, :, OFF_VA:OFF_VA + 384]),
                               np.ascontiguousarray(wr[:, :, OFF_VB:OFF_VB + 384])], 0)
    d["wfa%d" % li] = np.ascontiguousarray(wr[:, :, OFF_FA:OFF_FA + NH])
    d["wconv%d" % li] = np.ascontiguousarray(np.concatenate([wr[:, :, OFF_GV:OFF_GV + 256], wr[:, :, OFF_GG:OFF_GG + 256]], 2))
    d["wo%d" % li] = _chunk_rows(np.asarray(inp["w_o"][l], np.float32))
    wup = _chunk_rows(np.asarray(inp["w_up"][l], np.float32))
    d["wup%d" % li] = np.stack([np.concatenate([wup[:, :, i * 128:(i + 1) * 128],
                                                wup[:, :, D_FF + i * 128:D_FF + (i + 1) * 128]], 2) for i in range(NFF)], 0)
    d["wdown%d" % li] = _chunk_rows(np.asarray(inp["w_down"][l], np.float32))
    d["ln1_%d" % li] = _vec_chunks(np.asarray(inp["ln1_g"][l], np.float32))
    d["ln2_%d" % li] = _vec_chunks(np.asarray(inp["ln2_g"][l], np.float32))
    d["gfox%d" % li] = _vec_chunks(np.asarray(inp["g_out_fox"][l], np.float32))
    d["gdil%d" % li] = _vec_chunks(np.asarray(inp["g_out_dil"][l], np.float32))
    cw = np.asarray(inp["conv_w"][l], np.float32)
    d["convw%d" % li] = np.ascontiguousarray(cw.T.reshape(2, 128, CONV_K).transpose(1, 0, 2))
    d["convb%d" % li] = _vec_chunks(np.asarray(inp["conv_b"][l], np.float32))
    d["cng%d" % li] = _vec_chunks(np.asarray(inp["cnorm_g"][l], np.float32))
    d["cnb%d" % li] = _vec_chunks(np.asarray(inp["cnorm_b"][l], np.float32))
    fw = np.asarray(inp["ffn_conv_w"][l], np.float32)
    d["ffnw%d" % li] = np.ascontiguousarray(fw.T.reshape(2 * NFF, 128, 3).transpose(1, 0, 2))
    d["ffnb%d" % li] = _vec_chunks(np.asarray(inp["ffn_conv_b"][l], np.float32))
    d["bf%d" % li] = np.ascontiguousarray(np.asarray(inp["b_forget"][l], np.float32).reshape(NH, 1))
    return d


def x_to_dev(x):
    n = x.shape[0]
    return np.ascontiguousarray(x.transpose(0, 2, 1).reshape(n, KD, 128, T))


def x_from_dev(y):
    n = y.shape[0]
    return np.ascontiguousarray(y.reshape(n, D_MODEL, T).transpose(0, 2, 1))


_PROG_CACHE = {}


def get_prog(key):
    if key not in _PROG_CACHE:
        _PROG_CACHE[key] = build_program(*key)
    return _PROG_CACHE[key]


def kernel(**inputs):
    x = np.asarray(inputs["x"], np.float32)
    B = x.shape[0]
    per = B // N_CORES
    consts = make_consts()
    depth = inputs["w_in"].shape[0]
    common = dict(consts)
    common["g_final"] = _vec_chunks(np.asarray(inputs["g_final"], np.float32))
    for l in range(depth):
        common.update(prep_layer(inputs, l, l))
    nc = get_prog((per, depth, True, True, True))
    xd = x_to_dev(x)
    in_maps = []
    for c in range(N_CORES):
        m = dict(common)
        m["xT"] = xd[c * per:(c + 1) * per]
        in_maps.append(m)
    res = run_bass_kernel_spmd(nc, in_maps, core_ids=list(range(N_CORES)))
    y = np.concatenate([np.asarray(r["yT"]) for r in res.results], 0)
    return x_from_dev(y).astype(np.float32)
```
